# Optimizing a Trainium2 kernel written in Bass

```python
import jax, jax.numpy as jnp
from jax import lax
import numpy as np


D_MODEL = 1024
BATCH = 16
SEQ = 2048
DEPTH = 1

HEAD_DIM = 64
RWKV_HEADS = 8
RWKV_WIDTH = RWKV_HEADS * HEAD_DIM
DECAY_LORA = 32
AAA_LORA = 32
ATTN_HEADS = 8
ATTN_WIDTH = ATTN_HEADS * HEAD_DIM
MIX_WIDTH = RWKV_WIDTH + ATTN_WIDTH
Q_RANK = 256
KV_RANK = 128
IDX_HEADS = 4
IDX_DIM = 64
TOPK_MAX = 256
Q_BLOCK = 128
NORM_EPS = 1e-6
LN_EPS = 1e-5
GN_EPS = 64e-5

SHIFT_SIZES = [RWKV_WIDTH, RWKV_WIDTH, RWKV_WIDTH, DECAY_LORA, AAA_LORA]
SHIFT_COLS = sum(SHIFT_SIZES)
IN_SIZES = [SHIFT_COLS, RWKV_WIDTH, Q_RANK, KV_RANK, IDX_DIM, IDX_HEADS, ATTN_WIDTH]
IN_COLS = sum(IN_SIZES)
SHIFT_SPLITS = np.cumsum(SHIFT_SIZES)[:-1].tolist()
IN_SPLITS = np.cumsum(IN_SIZES)[:-1].tolist()

kernel_name = "hymba_rwkv7_dsa_hybrid"


def rmsnorm(x, g):
    xf = x.astype(jnp.float32)
    y = xf * lax.rsqrt(jnp.mean(xf * xf, axis=-1, keepdims=True) + NORM_EPS)
    return (y * g.astype(jnp.float32)).astype(x.dtype)


def layernorm(x, g, b):
    xf = x.astype(jnp.float32)
    mu = jnp.mean(xf, axis=-1, keepdims=True)
    var = jnp.mean(jnp.square(xf - mu), axis=-1, keepdims=True)
    y = (xf - mu) * lax.rsqrt(var + LN_EPS)
    return (y * g.astype(jnp.float32) + b.astype(jnp.float32)).astype(x.dtype)


def rwkv7_scan(r, decay, k, v, kk, a):
    B, T, H, D = r.shape
    tm = lambda t: jnp.moveaxis(t, 1, 0)

    def step(S, inp):
        r_t, w_t, k_t, v_t, kk_t, a_t = inp
        sa = jnp.einsum('bhij,bhj->bhi', S, -kk_t)
        S = (S * w_t[:, :, None, :]
             + sa[..., None] * (kk_t * a_t)[:, :, None, :]
             + v_t[..., None] * k_t[:, :, None, :])
        return S, jnp.einsum('bhij,bhj->bhi', S, r_t)

    S0 = jnp.zeros((B, H, D, D), jnp.float32)
    _, y = lax.scan(step, S0, (tm(r), tm(decay), tm(k), tm(v), tm(kk), tm(a)))
    return jnp.moveaxis(y, 0, 1)


def rwkv7_branch(p_shift, gate, mu_shift, w0, w_up, a0, a_up, k_k, k_a, r_k, gn_g, gn_b):
    B, T, _ = p_shift.shape
    f32 = jnp.float32
    prev = jnp.pad(p_shift, ((0, 0), (1, 0), (0, 0)))[:, :-1]
    xs = p_shift + (prev - p_shift) * mu_shift
    r, k, v, wd, ad = jnp.split(xs, SHIFT_SPLITS, axis=-1)
    wlog = -jax.nn.softplus(-(w0 + jnp.tanh(wd) @ w_up)) - 0.5
    decay = jnp.exp(-jnp.exp(wlog.astype(f32)))
    a = jax.nn.sigmoid((a0 + ad @ a_up).astype(f32))
    heads = lambda t: t.astype(f32).reshape(B, T, RWKV_HEADS, HEAD_DIM)
    r, k, v, a, decay = heads(r), heads(k), heads(v), heads(a), heads(decay)
    kk = k * k_k.astype(f32)
    kk = kk / jnp.maximum(jnp.sqrt(jnp.sum(kk * kk, axis=-1, keepdims=True)), 1e-12)
    k = k * (1.0 + (a - 1.0) * k_a.astype(f32))
    y = rwkv7_scan(r, decay, k, v, kk, a)
    mu = jnp.mean(y, axis=-1, keepdims=True)
    var = jnp.mean(jnp.square(y - mu), axis=-1, keepdims=True)
    y = (y - mu) * lax.rsqrt(var + GN_EPS) * gn_g.astype(f32) + gn_b.astype(f32)
    y = y + jnp.sum(r * k * r_k.astype(f32), axis=-1, keepdims=True) * v
    y = y.reshape(B, T, RWKV_WIDTH).astype(gate.dtype)
    return y * jax.nn.silu(gate)


def dsa_branch(q_down, kv_down, k_idx_raw, w_idx, gate, q_norm_g, kv_norm_g,
               w_uq, w_uk, w_uv, w_qidx, kidx_g, kidx_b):
    B, T, _ = q_down.shape
    f32 = jnp.float32
    c_q = rmsnorm(q_down, q_norm_g)
    c_kv = rmsnorm(kv_down, kv_norm_g)
    q = (c_q @ w_uq).reshape(B, T, ATTN_HEADS, HEAD_DIM)
    q_abs = jnp.einsum('bthd,hrd->bthr', q, w_uk) * (HEAD_DIM ** -0.5)
    q_idx = (c_q @ w_qidx).reshape(B, T, IDX_HEADS, IDX_DIM)
    k_idx = layernorm(k_idx_raw, kidx_g, kidx_b).astype(f32)
    w_i = w_idx * (IDX_HEADS ** -0.5)
    topk = min(TOPK_MAX, T // 4)
    nb = T // Q_BLOCK
    blocks = lambda t: jnp.moveaxis(t.reshape(B, nb, Q_BLOCK, *t.shape[2:]), 1, 0)
    key_pos = jnp.arange(T)

    def attend(args):
        qa, qi, wi, t0 = args
        q_pos = t0 + jnp.arange(Q_BLOCK)
        causal = key_pos[None, :] <= q_pos[:, None]
        logits = jnp.einsum('bqhd,bsd->bqhs', qi.astype(f32), k_idx)
        score = jnp.einsum('bqh,bqhs->bqs', wi.astype(f32), jax.nn.relu(logits))
        score = jnp.where(causal[None], score, -jnp.inf)
        _, idx = lax.top_k(score, topk)
        c_sel = jax.vmap(lambda c, i: c[i])(c_kv, idx).astype(f32)
        valid = idx <= q_pos[None, :, None]
        s = jnp.einsum('bqhr,bqkr->bhqk', qa.astype(f32), c_sel)
        s = jnp.where(valid[:, None], s, -jnp.inf)
        p = jax.nn.softmax(s, axis=-1)
        return jnp.einsum('bhqk,bqkr->bqhr', p, c_sel)

    o_lat = lax.map(attend, (blocks(q_abs), blocks(q_idx), blocks(w_i), jnp.arange(nb) * Q_BLOCK))
    o_lat = jnp.moveaxis(o_lat, 0, 1).reshape(B, T, ATTN_HEADS, KV_RANK).astype(gate.dtype)
    o = jnp.einsum('bthr,hrd->bthd', o_lat, w_uv).reshape(B, T, ATTN_WIDTH)
    return o * jax.nn.silu(gate)


def hybrid_layer(x, norm_g, w_in, mu_shift, w0, w_up, a0, a_up, k_k, k_a, r_k, gn_g, gn_b,
                 q_norm_g, kv_norm_g, w_uq, w_uk, w_uv, w_qidx, kidx_g, kidx_b, w_out):
    xn = rmsnorm(x, norm_g)
    p = xn @ w_in
    p_shift, g_rwkv, q_down, kv_down, k_idx_raw, w_idx, g_attn = jnp.split(p, IN_SPLITS, axis=-1)
    y_rwkv = rwkv7_branch(p_shift, g_rwkv, mu_shift, w0, w_up, a0, a_up, k_k, k_a, r_k, gn_g, gn_b)
    y_attn = dsa_branch(q_down, kv_down, k_idx_raw, w_idx, g_attn, q_norm_g, kv_norm_g,
                        w_uq, w_uk, w_uv, w_qidx, kidx_g, kidx_b)
    y = jnp.concatenate([y_rwkv, y_attn], axis=-1) @ w_out
    return x + y


def setup_inputs(seed: int = 0) -> dict:
    key = jax.random.key(seed)
    ks = jax.random.split(key, 24)
    L = DEPTH
    f32 = jnp.float32
    nrm = lambda k, shape, scale: scale * jax.random.normal(k, shape, f32)
    gain = lambda k, shape: 1.0 + 0.05 * jax.random.normal(k, shape, f32)
    hd = (L, RWKV_HEADS, HEAD_DIM)
    return {
        "x": nrm(ks[0], (BATCH, SEQ, D_MODEL), 1.0),
        "norm_g": gain(ks[1], (L, D_MODEL)),
        "w_in": nrm(ks[2], (L, D_MODEL, IN_COLS), D_MODEL ** -0.5),
        "mu_shift": jax.random.uniform(ks[3], (L, SHIFT_COLS), f32),
        "w0": jax.random.uniform(ks[4], (L, RWKV_WIDTH), f32, -2.0, 2.0),
        "w_up": nrm(ks[5], (L, DECAY_LORA, RWKV_WIDTH), 0.5 * DECAY_LORA ** -0.5),
        "a0": nrm(ks[6], (L, RWKV_WIDTH), 0.5),
        "a_up": nrm(ks[7], (L, AAA_LORA, RWKV_WIDTH), 0.5 * AAA_LORA ** -0.5),
        "k_k": 0.85 + 0.05 * jax.random.normal(ks[8], hd, f32),
        "k_a": gain(ks[9], hd),
        "r_k": nrm(ks[10], hd, 0.1),
        "gn_g": gain(ks[11], hd),
        "gn_b": nrm(ks[12], hd, 0.02),
        "q_norm_g": gain(ks[13], (L, Q_RANK)),
        "kv_norm_g": gain(ks[14], (L, KV_RANK)),
        "w_uq": nrm(ks[15], (L, Q_RANK, ATTN_WIDTH), Q_RANK ** -0.5),
        "w_uk": nrm(ks[16], (L, ATTN_HEADS, KV_RANK, HEAD_DIM), KV_RANK ** -0.5),
        "w_uv": nrm(ks[17], (L, ATTN_HEADS, KV_RANK, HEAD_DIM), KV_RANK ** -0.5),
        "w_qidx": nrm(ks[18], (L, Q_RANK, IDX_HEADS * IDX_DIM), Q_RANK ** -0.5),
        "kidx_g": gain(ks[19], (L, IDX_DIM)),
        "kidx_b": nrm(ks[20], (L, IDX_DIM), 0.02),
        "w_out": nrm(ks[21], (L, MIX_WIDTH, D_MODEL), MIX_WIDTH ** -0.5),
        "final_g": gain(ks[22], (D_MODEL,)),
    }


def reference(x, norm_g, w_in, mu_shift, w0, w_up, a0, a_up, k_k, k_a, r_k, gn_g, gn_b,
              q_norm_g, kv_norm_g, w_uq, w_uk, w_uv, w_qidx, kidx_g, kidx_b, w_out, final_g):
    for l in range(DEPTH):
        x = hybrid_layer(x, norm_g[l], w_in[l], mu_shift[l], w0[l], w_up[l], a0[l], a_up[l],
                         k_k[l], k_a[l], r_k[l], gn_g[l], gn_b[l], q_norm_g[l], kv_norm_g[l],
                         w_uq[l], w_uk[l], w_uv[l], w_qidx[l], kidx_g[l], kidx_b[l], w_out[l])
    return rmsnorm(x, final_g)
```

```python
from contextlib import ExitStack
import math
import numpy as np
import concourse.bass as bass
import concourse.mybir as mybir
from concourse.bass_utils import run_bass_kernel_spmd

F32 = mybir.dt.float32
BF16 = mybir.dt.bfloat16
ALU = mybir.AluOpType
AF = mybir.ActivationFunctionType
AX = mybir.AxisListType

ENGS = ("pe", "act", "dve", "pool", "sp")
KAPPA = math.exp(-0.5)
NEG = -1.0e30
NCT = 27
TOPK = 256
PE_MODE_DRAIN = False
USE_MARKS = False
MERGE_MODE = "prop"
NO_SELF_SYNC = ()

C_MU, C_W0, C_A0, C_KK, C_KA, C_RK, C_NG, C_QG, C_KVG, C_KIG, C_KIB, NCOLS = 0, 14, 18, 22, 26, 30, 34, 42, 44, 45, 46, 47


class Prog:
    def __init__(self, nc, es, n_dma_sems=12):
        self.nc = nc
        self.es = es
        self.ops = {e: [] for e in ENGS}
        self.cnt = {e: 0 for e in ENGS}
        self.sems = {}
        for e in ENGS:
            self.sems[e] = es.enter_context(nc.semaphore("s_" + e))
        for i in range(n_dma_sems):
            k = "dma%d" % i
            self.sems[k] = es.enter_context(nc.semaphore("s_" + k))
            self.cnt[k] = 0
        self.seen = {e: {} for e in ENGS}
        self.last_w = {}
        self.readers = {}
        self.pe_mode = None
        self.pe_group = False
        self.pe_stream = -1
        self.cur_stream = -1
        self.rec = None
        self.final_toks = []
        self.burst_stack = []

    def sb(self, name, shape, dt=F32):
        return self.es.enter_context(self.nc.sbuf_tensor("sb_" + name, list(shape), dt))

    def ps(self, name, shape, dt=F32):
        return self.es.enter_context(self.nc.psum_tensor("ps_" + name, list(shape), dt))

    def record(self, fn):
        assert self.rec is None
        self.rec = []
        fn()
        r, self.rec = self.rec, None
        return r

    def burst_begin(self):
        if self.rec is not None:
            self.burst_stack.append(self.rec)
            self.rec = []

    def burst_end(self):
        if self.rec is not None and self.burst_stack:
            inner = self.rec
            self.rec = self.burst_stack.pop()
            if inner:
                tot = sum(o[1].get("w", 1.0) for o in inner)
                first = inner[0]
                kw = dict(first[1])
                kw["w"] = tot
                kw["burst"] = inner
                self.rec.append((first[0], kw))

    def mark(self):
        if self.rec is not None and USE_MARKS:
            self.rec.append("MARK")

    def merge(self, a, b):
        def split(x):
            segs = [[]]
            for o in x:
                if o == "MARK":
                    segs.append([])
                else:
                    segs[-1].append(o)
            return segs
        sa, sb = split(a), split(b)
        while len(sa) < len(sb):
            sa.append([])
        while len(sb) < len(sa):
            sb.append([])
        for xa, xb in zip(sa, sb):
            self._merge_seg(xa, xb)

    def merge_global(self, chunks, lag):
        A, Bq = [], []
        for g, (a, b) in enumerate(chunks):
            for x, dst, off in ((a, A, 0.0), (b, Bq, lag)):
                tot = sum(o[1].get("w", 1.0) for o in x) or 1.0
                acc = 0.0
                for o in x:
                    w = o[1].get("w", 1.0)
                    dst.append((g + off + (acc + 0.5 * w) / tot, o))
                    acc += w
        wr_a = set(k for _, o in A for k in o[1]["writes"])
        rd_b = set(k for _, o in Bq for k in o[1]["reads"])
        shared = wr_a & rd_b
        ia = ib = 0
        rd_pos = {}
        for j, (kk, o) in enumerate(Bq):
            for k in o[1]["reads"]:
                if k in shared:
                    rd_pos.setdefault(k, []).append((int(kk - lag + 1e-9), j))
        while ia < len(A) or ib < len(Bq):
            if ia >= len(A):
                take_b = True
            elif ib >= len(Bq):
                take_b = False
            else:
                ka, oa = A[ia]
                need = -1
                for k in oa[1]["writes"]:
                    for (gb, j) in rd_pos.get(k, ()):
                        if gb <= int(ka) and j >= ib:
                            need = max(need, j)
                take_b = need >= ib or Bq[ib][0] < ka
            _, o = (Bq[ib] if take_b else A[ia])
            if take_b:
                ib += 1
            else:
                ia += 1
            self.cur_stream = 1 if take_b else 0
            self.op(*o[0], **o[1])
        self.cur_stream = -1

    def _merge_seg(self, a, b):
        if not a or not b:
            for o in a + b:
                self.op(*o[0], **o[1])
            return
        if MERGE_MODE == "sim":
            return self._merge_sim(a, b)
        def keyed(x, tag):
            tot = sum(o[1].get("w", 1.0) for o in x)
            acc, out = 0.0, []
            for i, o in enumerate(x):
                w = o[1].get("w", 1.0)
                out.append(((acc + 0.5 * w) / tot, tag, i, o))
                acc += w
            return out
        items = keyed(a, 0) + keyed(b, 1)
        items.sort(key=lambda t: (t[0], t[1], t[2]))
        for _, _, _, o in items:
            self.op(*o[0], **o[1])

    def _merge_sim(self, a, b):
        DEF = {"pe": 0.6, "act": 0.5, "dve": 0.5, "pool": 0.9, "sp": 0.15}
        wr_a = set(k for o in a for k in o[1]["writes"])
        rd_b = set(k for o in b for k in o[1]["reads"])
        wr_b = set(k for o in b for k in o[1]["writes"])
        rd_a = set(k for o in a for k in o[1]["reads"])
        shared = wr_a & rd_b
        assert not ((wr_b & rd_a) - shared), sorted((wr_b & rd_a) - shared)
        last_rd_b = {}
        for j, o in enumerate(b):
            for k in o[1]["reads"]:
                if k in shared:
                    last_rd_b[k] = j
        eng_free, ready_w, ready_r = {}, {}, {}
        rem = [sum(o[1].get("dur") or DEF[o[0][0]] for o in a), sum(o[1].get("dur") or DEF[o[0][0]] for o in b)]
        ptr = [0, 0]
        streams = [a, b]

        def est(o):
            (eng, _), kw = o
            t = eng_free.get(eng, 0.0)
            for k in kw["reads"]:
                t = max(t, ready_w.get(k, 0.0))
            for k in kw["writes"]:
                t = max(t, ready_w.get(k, 0.0), ready_r.get(k, 0.0))
            return t

        def emit(si):
            o = streams[si][ptr[si]]
            ptr[si] += 1
            (eng, _), kw = o
            d = kw.get("dur") or DEF[eng]
            st = est(o)
            if kw["dma"] is not None:
                eng_free[eng] = st + 0.15
                done = st + d
            else:
                eng_free[eng] = st + d
                done = st + d + 0.25
            for k in kw["reads"]:
                ready_r[k] = max(ready_r.get(k, 0.0), done)
            for k in kw["writes"]:
                ready_w[k] = done
            rem[si] -= d
            kw2 = dict(kw)
            kw2.pop("dur", None)
            self.op(*o[0], **kw2)

        while ptr[0] < len(a) or ptr[1] < len(b):
            if ptr[0] >= len(a):
                emit(1)
                continue
            if ptr[1] >= len(b):
                emit(0)
                continue
            oa = a[ptr[0]]
            need = max([last_rd_b.get(k, -1) for k in oa[1]["writes"]] + [-1])
            if need >= ptr[1]:
                emit(1)
                continue
            ta, tb = est(oa), est(b[ptr[1]])
            if ta < tb or (ta == tb and rem[0] >= rem[1]):
                emit(0)
            else:
                emit(1)

    def op(self, eng, fn, reads=(), writes=(), dma=None, mode=None, accum=False, final=False, w=1.0, dur=None, group=False, burst=None):
        if burst is not None:
            for o in burst:
                self.op(*o[0], **o[1])
            return None
        if self.rec is not None:
            self.rec.append(((eng, fn), dict(reads=list(reads), writes=list(writes), dma=dma, mode=mode, accum=accum, final=final, w=w, dur=dur, group=group)))
            return None
        deps = {}

        def add(tok):
            s, v = tok
            if deps.get(s, 0) < v:
                deps[s] = v
        for k in reads:
            if k in self.last_w:
                add(self.last_w[k])
        for k in writes:
            if k in self.last_w:
                add(self.last_w[k])
            for t in self.readers.get(k, ()):
                add(t)
        waits = []
        if eng == "pe":
            if mode != self.pe_mode and self.cnt["pe"] > 0 and (PE_MODE_DRAIN or ((group or self.pe_group) and self.cur_stream != self.pe_stream)):
                deps["pe"] = self.cnt["pe"]
            elif "pe" in deps and accum:
                del deps["pe"]
            self.pe_mode = mode
            self.pe_group = group
            self.pe_stream = self.cur_stream
        for s, v in deps.items():
            if s == eng and eng in NO_SELF_SYNC:
                continue
            if self.seen[eng].get(s, 0) >= v:
                continue
            self.seen[eng][s] = v
            waits.append((s, v))
        if dma is None:
            self.cnt[eng] += 1
            tok = (eng, self.cnt[eng])
            inc = 1
        else:
            self.cnt[dma] += 16
            tok = (dma, self.cnt[dma])
            inc = 16
        self.ops[eng].append((waits, fn, tok[0], inc))
        for k in reads:
            self.readers.setdefault(k, []).append(tok)
        for k in writes:
            self.last_w[k] = tok
            self.readers[k] = []
        if final:
            self.final_toks.append(tok)
        return tok

    def finish(self):
        nc = self.nc
        fin = {}
        for s, v in self.final_toks:
            fin[s] = max(fin.get(s, 0), v)
        with nc.Block() as block:
            def replay(e, engname):
                for waits, fn, semk, inc in self.ops[engname]:
                    for s, v in waits:
                        e.wait_ge(self.sems[s], v)
                    ins = fn(e)
                    ins.then_inc(self.sems[semk], inc)

            @block.tensor
            def _(e):
                replay(e, "pe")

            @block.scalar
            def _(e):
                replay(e, "act")

            @block.vector
            def _(e):
                replay(e, "dve")

            @block.gpsimd
            def _(e):
                replay(e, "pool")

            @block.sync
            def _(e):
                replay(e, "sp")
                for s, v in fin.items():
                    e.wait_ge(self.sems[s], v)


def build(NSEQ, NCH, debug=None):
    debug = debug or {}
    T = NCH * 128
    NG = NSEQ * NCH
    nc = bass.Bass("TRN2", target_bir_lowering=False)
    D_ = lambda name, shape: nc.dram_tensor(name, list(shape), F32, kind="ExternalInput").ap()
    x_d = D_("x", [NSEQ * T, 1024])
    win_d = D_("w_in_r", [NCT, 128, 8, 128])
    cols_d = D_("cols", [128, NCOLS])
    bcv_d = D_("bcv", [128, 2048])
    wup_d = D_("w_up", [32, 512])
    aup_d = D_("a_up", [32, 512])
    wuq_d = D_("w_uq_r", [128, 2, 512])
    wqi_d = D_("w_qidx_r", [128, 2, 256])
    wuk_d = D_("w_ukT_r", [128, 4, 128])
    wuv_d = D_("w_uv_r", [128, 8, 64])
    wout_d = D_("w_out_r", [128, 8, 1024])
    out_d = nc.dram_tensor("out", [NSEQ * T, 1024], F32, kind="ExternalOutput").ap()
    dbg_d = {}

    with ExitStack() as es:
        P = Prog(nc, es)
        op = P.op
        ident = P.sb("ident", [128, 128])
        ones = P.sb("ones", [128, 128])
        bones = P.sb("bones", [128, 128])
        hsel = P.sb("hsel", [128, 2])
        mSI2 = P.sb("mSI2", [128, 512])
        mSL = P.sb("mSL", [128, 128])
        mQS = P.sb("mQS", [128, 128])
        cols = P.sb("cols", [128, NCOLS])
        omka = P.sb("omka", [128, 4])
        bcv = P.sb("bcv", [128, 2048])
        wup_t = P.sb("wup", [32, 512])
        aup_t = P.sb("aup", [32, 512])
        wup, aup = wup_t[:], aup_t[:]
        wuq = P.sb("wuq", [128, 2, 512])
        wqi = P.sb("wqi", [128, 2, 256])
        wuk = P.sb("wuk", [128, 4, 128])
        wuv = P.sb("wuv", [128, 8, 64])
        NWO = 3
        wosl = [P.sb("wosl%d" % i, [128, 1024]) for i in range(NWO)]
        xT = P.sb("xT", [128, 8, 128])
        NSLOT = 3
        wsl = [P.sb("wsl%d" % i, [128, 8, 128]) for i in range(NSLOT)]
        pT = P.sb("pT", [128, NCT, 129])
        sm = P.sb("sm", [128, 16])
        S = P.sb("S", [128, 4, 64])
        S0m = P.sb("S0m", [128, 4, 64])
        gm = P.sb("gm", [128, 4])
        gC = P.sb("gC", [128, 4])
        c63 = P.sb("c63", [128, 4])
        NA = [P.sb("NA%d" % h, [128, 384]) for h in range(8)]
        TT = [P.sb("TT%d" % h, [128, 128]) for h in range(8)]
        AAk = [[P.sb("AA%d_%d" % (s, q), [128, 256]) for q in range(2)] for s in range(4)]
        kidxT = P.sb("kidxT", [128, T])
        ckvT = P.sb("ckvT", [128, T])
        ckv_tm = P.sb("ckv_tm", [128, NCH, 128])
        mT = P.sb("mT", [128, 16, 128], BF16)
        m8 = P.sb("m8", [128, 8])
        wi = P.sb("wi", [128, 4])
        NZ = 42
        Z = P.sb("Z", [128, NZ, 512])
        zk = lambda a, b=None: ["Z%d" % i for i in range(a, (a + 1) if b is None else b)]
        ycat = Z[:, 40:42, :].rearrange("p a b -> p (a b)").rearrange("p (a b) -> p a b", a=8)
        YCK = zk(40, 42)

        def z4(i):
            return Z[:, i, :].rearrange("p (a b) -> p a b", a=4)

        def z2(i):
            return Z[:, i:i + 2, :].rearrange("p a b -> p (a b)")
        B = [P.ps("B%d" % i, [128, 512]) for i in range(8)]
        b4 = lambda i: B[i][:, :].rearrange("p (a b) -> p a b", a=4)
        bk_ = lambda i: ["B%da" % i, "B%db" % i]

        ld = []
        for dst, src, key in [(cols[:], cols_d, "cols"), (bcv[:], bcv_d, "bcv"), (wup, wup_d, "wup"), (aup, aup_d, "aup"),
                              (wuq[:], wuq_d, "wuq"), (wqi[:], wqi_d, "wqi"), (wuk[:], wuk_d, "wuk"), (wuv[:], wuv_d, "wuv")]:
            ld.append((key, op("sp", lambda e, dst=dst, src=src: e.dma_start(out=dst, in_=src), writes=[key], dma="dma0")))
        for key, _ in ld:
            P.last_w[key] = ld[-1][1]
        op("pool", lambda e: e.memset(ones[:], 1.0), writes=["ones"])
        op("pool", lambda e: e.affine_select(out=ident[:], in_=ones[:], pattern=[[-1, 128]], compare_op=ALU.is_equal,
                                             fill=0.0, base=0, channel_multiplier=1), reads=["ones"], writes=["ident"])
        for q in range(4):
            cmp_ = ALU.is_gt if q % 2 == 0 else ALU.is_ge
            op("pool", lambda e, q=q, cmp_=cmp_: e.affine_select(out=mSI2[:, q * 128:(q + 1) * 128], in_=ones[:], pattern=[[1, 128]],
                                                                 compare_op=cmp_, fill=0.0, base=0, channel_multiplier=-1),
               reads=["ones"], writes=["mSI2"])
        op("pool", lambda e: e.affine_select(out=mSL[:], in_=ones[:], pattern=[[-1, 128]], compare_op=ALU.is_gt,
                                             fill=0.0, base=0, channel_multiplier=1), reads=["ones"], writes=["mSL"])
        op("pool", lambda e: e.affine_select(out=mQS[:], in_=ones[:], pattern=[[-1, 128]], compare_op=ALU.is_ge,
                                             fill=0.0, base=0, channel_multiplier=1), reads=["ones"], writes=["mQS"])
        op("pool", lambda e: e.memset(bones[:], 0.0), writes=["bones"])
        op("pool", lambda e: e.memset(bones[0:64, 0:64], 1.0), writes=["bones"])
        op("pool", lambda e: e.memset(bones[64:128, 64:128], 1.0), writes=["bones"])
        op("pool", lambda e: e.memset(hsel[:], 0.0), writes=["hsel"])
        op("pool", lambda e: e.memset(hsel[0:64, 0:1], 1.0), writes=["hsel"])
        op("pool", lambda e: e.memset(hsel[64:128, 1:2], 1.0), writes=["hsel"])
        op("dve", lambda e: e.tensor_scalar(out=omka[:], in0=cols[:, C_KA:C_KA + 4], scalar1=-1.0, scalar2=1.0,
                                            op0=ALU.mult, op1=ALU.add), reads=["cols"], writes=["omka"])
        for kc in range(2):
            op("dve", lambda e, kc=kc: e.tensor_scalar(out=wuq[:, kc, :], in0=wuq[:, kc, :], scalar1=cols[:, C_QG + kc:C_QG + kc + 1],
                                                       scalar2=None, op0=ALU.mult), reads=["cols", "wuq"], writes=["wuq"])
            op("dve", lambda e, kc=kc: e.tensor_scalar(out=wqi[:, kc, :], in0=wqi[:, kc, :], scalar1=cols[:, C_QG + kc:C_QG + kc + 1],
                                                       scalar2=None, op0=ALU.mult), reads=["cols", "wqi"], writes=["wqi"])

        def dump(name, ap, keys, shape, g):
            if name in debug and (debug[name][0] * NCH + debug[name][1]) == g:
                d = nc.dram_tensor("dbg_" + name, list(shape), F32, kind="ExternalOutput").ap()
                dbg_d[name] = d
                op("sp", lambda e: e.dma_start(out=d, in_=ap), reads=keys, dma="dma11", final=True)

        TKW = 3.0
        TKP = 1.0
        PM_ENG = "dve"
        wtile_ctr = [0]
        wo_ctr = [0]
        rup = lambda n: 32 if n <= 32 else 64 if n <= 64 else 128

        def MM(out, lhsT, rhs, start, stop, reads, writes):
            mode = ("M", rup(lhsT.shape[0]), rup(lhsT.shape[-1]))
            return op("pe", lambda e: e.matmul(out, lhsT=lhsT, rhs=rhs, start=start, stop=stop), reads=reads, writes=writes,
                      mode=mode, accum=(not start), dur=0.3 + 0.0034 * rhs.shape[-1], group=not (start and stop))

        def TR(out, in_, identity, reads, writes):
            mode = ("T", rup(in_.shape[0]), rup(in_.shape[-1]))
            return op("pe", lambda e: e.transpose(out=out, in_=in_, identity=identity), reads=reads, writes=writes, mode=mode, dur=0.35)

        pk = lambda a, b: ["pT%d" % i for i in range(a, b)]

        def XP(g, bt=(2, 3), ba=(0, 1), zs=29, t0=0, t1=NCT):
            row0 = g * 128
            if t0 == 0:
                XP_head(g, bt, zs)
            for ct in range(t0, t1):
                slot = wtile_ctr[0] % NSLOT
                wtile_ctr[0] += 1
                op("sp", lambda e, ct=ct, slot=slot: e.dma_start(out=wsl[slot][:], in_=win_d[ct]), writes=["wsl%d" % slot],
                   dma="dma%d" % (2 + slot), dur=3.7)
                gi, r = (ct - t0) // 4, (ct - t0) % 4
                bk = ba[gi % 2]
                P.burst_begin()
                for kc in range(8):
                    MM(B[bk][:, r * 128:(r + 1) * 128], wsl[slot][:, kc, :], xT[:, kc, :], (kc == 0), (kc == 7),
                       reads=["wsl%d" % slot, "xT"], writes=bk_(bk))
                P.burst_end()
                if r == 3 or ct == t1 - 1:
                    c0_, n_ = ct - r, r + 1
                    op("act", lambda e, c0_=c0_, n_=n_, bk=bk: e.copy(out=pT[:, c0_:c0_ + n_, 1:129],
                                                                      in_=B[bk][:, 0:n_ * 128].rearrange("p (a b) -> p a b", a=n_)),
                       reads=bk_(bk), writes=["pT%d" % i for i in range(c0_, c0_ + n_)])

        def XP_head(g, bt, zs):
            row0 = g * 128
            xcb = z2(zs + 2)
            xkk = zk(zs + 2, zs + 4)
            op("sp", lambda e: e.dma_start(out=xcb, in_=x_d[row0:row0 + 128, :]), writes=xkk, dma="dma1")
            xs2 = z2(zs)
            zsk = zk(zs, zs + 2)
            op("act", lambda e: e.activation(out=xs2, in_=xcb, func=AF.Square, accum_out=sm[:, 0:1]), reads=xkk, writes=zsk + ["sm0"])
            op("act", lambda e: e.activation(out=sm[:, 1:2], in_=sm[:, 0:1], func=AF.Sqrt, bias=1e-6, scale=1.0 / 1024),
               reads=["sm0"], writes=["sm1"])
            op("dve", lambda e: e.reciprocal(out=sm[:, 2:3], in_=sm[:, 1:2]), reads=["sm1"], writes=["sm2"])
            op("dve", lambda e: e.tensor_scalar(out=xs2, in0=xcb, scalar1=sm[:, 2:3], scalar2=None, op0=ALU.mult),
               reads=xkk + ["sm2"], writes=zsk)
            for hf in range(2):
                for q in range(4):
                    kc = hf * 4 + q
                    TR(B[bt[hf]][:, q * 128:(q + 1) * 128], xs2[:, kc * 128:(kc + 1) * 128], ident[:],
                       reads=zsk + ["ident"], writes=bk_(bt[hf]))
                op("dve", lambda e, hf=hf: e.tensor_tensor(out=xT[:, hf * 4:(hf + 1) * 4, :], in0=b4(bt[hf]),
                                                           in1=cols[:, C_NG + hf * 4:C_NG + hf * 4 + 4].unsqueeze(2).to_broadcast([128, 4, 128]),
                                                           op=ALU.mult), reads=bk_(bt[hf]) + ["cols"], writes=["xT"])

        def R(g, mid=None):
            c = g % NCH
            dmp = lambda name, ap, keys, shape: dump(name, ap, keys, shape, g)
            if c == 0:
                op("dve", lambda e: e.memset(S[:], 0.0), writes=["S"])
                op("dve", lambda e: e.memset(pT[:, 0:14, 0:1], 0.0), writes=pk(0, 14))
            dmp("pT", pT[:], pk(0, NCT), [128, NCT, 129])
            XS = Z[:, 0:4, :].rearrange("p a b -> p (a b)")[:, 0:14 * 128].rearrange("p (a b) -> p a b", a=14)
            XSk = zk(0, 4)
            mu_bc = cols[:, C_MU:C_MU + 14].unsqueeze(2).to_broadcast([128, 14, 128])
            op("dve", lambda e: e.tensor_tensor(out=XS, in0=pT[:, 0:14, 0:128], in1=pT[:, 0:14, 1:129], op=ALU.subtract),
               reads=pk(0, 14), writes=XSk)
            op("dve", lambda e: e.tensor_tensor(out=XS, in0=XS, in1=mu_bc, op=ALU.mult), reads=XSk + ["cols"], writes=XSk)
            op("dve", lambda e: e.tensor_tensor(out=XS, in0=XS, in1=pT[:, 0:14, 1:129], op=ALU.add), reads=XSk + pk(0, 14), writes=XSk)
            op("dve", lambda e: e.tensor_copy(out=pT[:, 0:14, 0:1], in_=pT[:, 0:14, 128:129]), reads=pk(0, 14), writes=pk(0, 14))
            Gs = z4(24)
            op("act", lambda e: e.activation(out=Gs, in_=pT[:, 14:18, 1:129], func=AF.Silu), reads=pk(14, 18), writes=zk(24))
            xr, xk_, xv = XS[:, 0:4, :], XS[:, 4:8, :], XS[:, 8:12, :]
            op("act", lambda e: e.activation(out=Z[0:32, 4, 0:128], in_=XS[0:32, 12, :], func=AF.Tanh), reads=XSk, writes=zk(4))
            for q in range(4):
                MM(B[4][:, q * 128:(q + 1) * 128], wup[:, q * 128:(q + 1) * 128], Z[0:32, 4, 0:128], True, True,
                   reads=["wup"] + zk(4), writes=bk_(4))
            for q in range(4):
                MM(B[5][:, q * 128:(q + 1) * 128], aup[:, q * 128:(q + 1) * 128], XS[0:32, 13, :], True, True,
                   reads=["aup"] + XSk, writes=bk_(5))
            sg, a_t, cs, E1, Einv, Ehat, kk, T1, bb = (z4(i) for i in (5, 6, 7, 8, 9, 10, 11, 12, 13))
            for q in range(4):
                op("act", lambda e, q=q: e.activation(out=sg[:, q, :], in_=B[4][:, q * 128:(q + 1) * 128], func=AF.Sigmoid,
                                                      bias=cols[:, C_W0 + q:C_W0 + q + 1]), reads=bk_(4) + ["cols"], writes=zk(5))
            for q in range(4):
                op("act", lambda e, q=q: e.activation(out=a_t[:, q, :], in_=B[5][:, q * 128:(q + 1) * 128], func=AF.Sigmoid,
                                                      bias=cols[:, C_A0 + q:C_A0 + q + 1]), reads=bk_(5) + ["cols"], writes=zk(6))
            for q in range(4):
                op("dve", lambda e, q=q: e.tensor_tensor_scan(out=cs[:, q, :], data0=ones[:, 0:1].to_broadcast([128, 128]), data1=sg[:, q, :],
                                                              initial=0.0, op0=ALU.mult, op1=ALU.add), reads=["ones"] + zk(5), writes=zk(7))
            op("act", lambda e: e.activation(out=gm[:].unsqueeze(2), in_=cs[:, :, 63:64], func=AF.Exp, scale=-KAPPA), reads=zk(7), writes=["gm"])
            op("dve", lambda e: e.tensor_copy(out=c63[:].unsqueeze(2), in_=cs[:, :, 63:64]), reads=zk(7), writes=["c63"])
            op("dve", lambda e: e.tensor_tensor(out=cs, in0=cs, in1=c63[:].unsqueeze(2).to_broadcast([128, 4, 128]), op=ALU.subtract),
               reads=zk(7) + ["c63"], writes=zk(7))
            op("dve", lambda e: e.tensor_tensor(out=sg, in0=cs, in1=sg, op=ALU.subtract), reads=zk(7) + zk(5), writes=zk(5))
            op("act", lambda e: e.activation(out=E1, in_=cs, func=AF.Exp, scale=-KAPPA), reads=zk(7), writes=zk(8))
            op("act", lambda e: e.activation(out=Einv, in_=cs, func=AF.Exp, scale=KAPPA), reads=zk(7), writes=zk(9))
            E0 = sg
            op("act", lambda e: e.activation(out=E0, in_=sg, func=AF.Exp, scale=-KAPPA), reads=zk(5), writes=zk(5))
            op("dve", lambda e: e.tensor_copy(out=gC[:].unsqueeze(2), in_=E1[:, :, 127:128]), reads=zk(8), writes=["gC"])
            op("dve", lambda e: e.tensor_tensor(out=Ehat, in0=Einv, in1=gC[:].unsqueeze(2).to_broadcast([128, 4, 128]), op=ALU.mult),
               reads=["gC"] + zk(9), writes=zk(10))
            bc4 = lambda c0: cols[:, c0:c0 + 4].unsqueeze(2).to_broadcast([128, 4, 128])
            op("dve", lambda e: e.tensor_tensor(out=kk, in0=xk_, in1=bc4(C_KK), op=ALU.mult), reads=XSk + ["cols"], writes=zk(11))
            op("pool", lambda e: e.tensor_tensor(out=T1, in0=kk, in1=kk, op=ALU.mult), reads=zk(11), writes=zk(12))
            for q in range(4):
                MM(B[4][:, q * 128:(q + 1) * 128], bones[:], T1[:, q, :], True, True, reads=["bones"] + zk(12), writes=bk_(4))
            op("act", lambda e: e.activation(out=T1, in_=b4(4), func=AF.Sqrt), reads=bk_(4), writes=zk(12))
            op("dve", lambda e: e.tensor_scalar(out=T1, in0=T1, scalar1=1e-12, scalar2=None, op0=ALU.max), reads=zk(12), writes=zk(12))
            op("dve", lambda e: e.reciprocal(out=T1, in_=T1), reads=zk(12), writes=zk(12))
            op("dve", lambda e: e.tensor_tensor(out=kk, in0=kk, in1=T1, op=ALU.mult), reads=zk(11) + zk(12), writes=zk(11))
            op("dve", lambda e: e.tensor_tensor(out=bb, in0=kk, in1=a_t, op=ALU.mult), reads=zk(11) + zk(6), writes=zk(13))
            op("dve", lambda e: e.tensor_tensor(out=T1, in0=a_t, in1=bc4(C_KA), op=ALU.mult), reads=zk(6) + ["cols"], writes=zk(12))
            op("dve", lambda e: e.tensor_tensor(out=T1, in0=T1, in1=omka[:].unsqueeze(2).to_broadcast([128, 4, 128]), op=ALU.add),
               reads=zk(12) + ["omka"], writes=zk(12))
            op("dve", lambda e: e.tensor_tensor(out=T1, in0=T1, in1=xk_, op=ALU.mult), reads=zk(12) + XSk, writes=zk(12))
            kmod = T1
            PR = z2(14).rearrange("p (a b c) -> p a b c", a=4, b=2)
            BK = z2(16).rearrange("p (a b c) -> p a b c", a=4, b=2)
            op("dve", lambda e: e.scalar_tensor_tensor(out=PR[:, :, 0, :], in0=kk, scalar=-1.0, in1=E0, op0=ALU.mult, op1=ALU.mult),
               reads=zk(11) + zk(5), writes=zk(14, 16))
            op("pool", lambda e: e.tensor_tensor(out=PR[:, :, 1, :], in0=xr, in1=E1, op=ALU.mult), reads=XSk + zk(8), writes=zk(14, 16))
            op("pool", lambda e: e.tensor_tensor(out=BK[:, :, 0, :], in0=bb, in1=Einv, op=ALU.mult), reads=zk(13) + zk(9), writes=zk(16, 18))
            op("pool", lambda e: e.tensor_tensor(out=BK[:, :, 1, :], in0=kmod, in1=Einv, op=ALU.mult), reads=zk(12) + zk(9), writes=zk(16, 18))
            BhT, KhT, rk = z4(18), z4(19), z4(20)
            op("pool", lambda e: e.tensor_tensor(out=BhT, in0=bb, in1=Ehat, op=ALU.mult), reads=zk(13) + zk(10), writes=zk(18))
            op("pool", lambda e: e.tensor_tensor(out=KhT, in0=kmod, in1=Ehat, op=ALU.mult), reads=zk(12) + zk(10), writes=zk(19))
            op("dve", lambda e: e.tensor_tensor(out=rk, in0=xr, in1=kmod, op=ALU.mult), reads=XSk + zk(12), writes=zk(20))
            op("dve", lambda e: e.tensor_tensor(out=rk, in0=rk, in1=bc4(C_RK), op=ALU.mult), reads=zk(20) + ["cols"], writes=zk(20))
            for src, srck, bank, dst in [(BhT, zk(18), 6, 21), (KhT, zk(19), 7, 22), (xv, XSk, 6, 23)]:
                for q in range(4):
                    TR(B[bank][:, q * 128:(q + 1) * 128], src[:, q, :], ident[:], reads=srck + ["ident"], writes=bk_(bank))
                op("act", lambda e, bank=bank, dst=dst: e.copy(out=Z[:, dst, :], in_=B[bank][:, :]), reads=bk_(bank), writes=zk(dst))
            Bh_tm, Kh_tm, V_tm = Z[:, 21, :], Z[:, 22, :], Z[:, 23, :]
            for q in range(4):
                MM(B[7][:, 2 * q:2 * q + 2], rk[:, q, :], hsel[:], True, True, reads=zk(20) + ["hsel"], writes=bk_(7))
            op("act", lambda e: e.copy(out=sm[:, 8:16], in_=B[7][:, 0:8]), reads=bk_(7), writes=["sm8"])
            def inv_head(h):
                q, pb = h // 2, (h % 2) * 64
                k = h % 4
                bk = 4 + k
                kB = bk_(bk)
                PRh = PR[pb:pb + 64, q, :, :].rearrange("p a b -> p (a b)")
                MM(B[bk][:, 0:256], BK[pb:pb + 64, q, 0, :], PRh, True, True, reads=zk(14, 18), writes=kB)
                MM(B[bk][:, 256:512], BK[pb:pb + 64, q, 1, :], PRh, True, True, reads=zk(14, 18), writes=kB)
                op("dve", lambda e: e.tensor_tensor(out=AAk[k][1][:, 128:256], in0=B[bk][:, 0:128], in1=mSI2[:, 0:128], op=ALU.mult),
                   reads=kB + ["mSI2"], writes=["AA%d_1" % k])
                op("dve", lambda e: e.tensor_tensor(out=NA[h][:], in0=B[bk][:, 128:512], in1=mSI2[:, 128:512], op=ALU.mult),
                   reads=kB + ["mSI2"], writes=["NA%d" % h])
                yield
                MM(B[bk][:, 0:128], PR[pb:pb + 64, q, 0, :], BK[pb:pb + 64, q, 0, :], True, True, reads=zk(14, 18), writes=kB)
                op("dve", lambda e: e.tensor_tensor(out=AAk[k][1][:, 0:128], in0=B[bk][:, 0:128], in1=mSL[:], op=ALU.mult),
                   reads=kB + ["mSL"], writes=["AA%d_1" % k])
                op("pool", lambda e: e.tensor_tensor(out=TT[h][:], in0=AAk[k][1][:, 128:256], in1=ident[:], op=ALU.add),
                   reads=["AA%d_1" % k, "ident"], writes=["TT%d" % h])
                yield
                A_cur, AT_cur = AAk[k][1][:, 0:128], AAk[k][1][:, 128:256]
                kA, kAT = "AA%d_1" % k, "AA%d_1" % k
                for lvl in range(6):
                    dstA = AAk[k][lvl % 2]
                    kD = "AA%d_%d" % (k, lvl % 2)
                    MM(B[bk][:, 0:128], AT_cur, A_cur, True, True, reads=[kA, kAT], writes=kB)
                    if lvl < 5:
                        MM(B[bk][:, 128:256], A_cur, AT_cur, True, True, reads=[kA, kAT], writes=kB)
                        op("act", lambda e, dstA=dstA: e.copy(out=dstA[:], in_=B[bk][:, 0:256]), reads=kB, writes=[kD])
                    else:
                        op("act", lambda e, dstA=dstA: e.copy(out=dstA[:, 0:128], in_=B[bk][:, 0:128]), reads=kB, writes=[kD])
                    yield
                    A_cur, AT_cur = dstA[:, 0:128], dstA[:, 128:256]
                    kA = kAT = kD
                    MM(B[bk][:, 256:384], A_cur, TT[h][:], True, True, reads=[kD, "TT%d" % h], writes=kB)
                    op("dve", lambda e: e.tensor_tensor(out=TT[h][:], in0=B[bk][:, 256:384], in1=TT[h][:], op=ALU.add),
                       reads=kB + ["TT%d" % h], writes=["TT%d" % h])
                    yield

            for grp in ((0, 1, 2, 3), (4, 5, 6, 7)):
                gens = [inv_head(h) for h in grp]
                while gens:
                    for gen in list(gens):
                        try:
                            next(gen)
                        except StopIteration:
                            gens.remove(gen)
            if mid is not None:
                mid()
            op("dve", lambda e: e.tensor_tensor(out=S0m[:], in0=S[:], in1=gm[:].unsqueeze(2).to_broadcast([128, 4, 64]), op=ALU.mult),
               reads=["S", "gm"], writes=["S0m"])
            X1s, Us, Ys, Ysq, STt = Z[:, 7, :], Z[:, 9, :], Z[:, 10, :], Z[:, 11, :], Z[:, 4, :]
            P.burst_begin()
            for h in range(8):
                q, pb = h // 2, (h % 2) * 64
                hs = slice(h * 64, (h + 1) * 64)
                MM(B[4][:, hs], PR[pb:pb + 64, q, 0, :], S0m[pb:pb + 64, q, :], True, False, reads=zk(14, 16) + ["S0m"], writes=bk_(4))
                MM(B[4][:, hs], NA[h][:, 128:256], V_tm[:, hs], False, True, reads=["NA%d" % h] + zk(23), writes=bk_(4))
            P.burst_end()
            op("act", lambda e: e.copy(out=X1s, in_=B[4][:, :]), reads=bk_(4), writes=zk(7))
            P.burst_begin()
            for h in range(8):
                hs = slice(h * 64, (h + 1) * 64)
                MM(B[5][:, hs], TT[h][:], X1s[:, hs], True, True, reads=["TT%d" % h] + zk(7), writes=bk_(5))
            P.burst_end()
            op("act", lambda e: e.copy(out=Us, in_=B[5][:, :]), reads=bk_(5), writes=zk(9))
            P.burst_begin()
            for h in range(8):
                q, pb = h // 2, (h % 2) * 64
                hs = slice(h * 64, (h + 1) * 64)
                MM(B[6][:, hs], PR[pb:pb + 64, q, 1, :], S0m[pb:pb + 64, q, :], True, False, reads=zk(14, 16) + ["S0m"], writes=bk_(6))
                MM(B[6][:, hs], NA[h][:, 0:128], Us[:, hs], False, False, reads=["NA%d" % h] + zk(9), writes=bk_(6))
                MM(B[6][:, hs], NA[h][:, 256:384], V_tm[:, hs], False, True, reads=["NA%d" % h] + zk(23), writes=bk_(6))
            P.burst_end()
            op("act", lambda e: e.copy(out=Ys, in_=B[6][:, :]), reads=bk_(6), writes=zk(10))
            P.burst_begin()
            for h in range(8):
                hs = slice(h * 64, (h + 1) * 64)
                MM(B[7][0:64, hs], Bh_tm[:, hs], Us[:, hs], True, False, reads=zk(21) + zk(9), writes=bk_(7))
                MM(B[7][0:64, hs], Kh_tm[:, hs], V_tm[:, hs], False, True, reads=zk(22) + zk(23), writes=bk_(7))
            P.burst_end()
            for h in range(8):
                q, pb = h // 2, (h % 2) * 64
                hs = slice(h * 64, (h + 1) * 64)
                op("dve", lambda e, q=q, pb=pb, hs=hs: e.scalar_tensor_tensor(out=S[pb:pb + 64, q, :], in0=S0m[pb:pb + 64, q, :],
                                                                              scalar=gC[pb:pb + 64, q:q + 1], in1=B[7][0:64, hs],
                                                                              op0=ALU.mult, op1=ALU.add),
                   reads=["S0m", "gC"] + bk_(7), writes=["S"])
            dmp("Ys", Ys, zk(10), [128, 512])
            Y3 = Ys.rearrange("p (a b) -> p a b", a=8)
            op("dve", lambda e: e.tensor_reduce(out=STt[:, 0:8], in_=Y3, axis=AX.X, op=ALU.add), reads=zk(10), writes=zk(4))
            op("act", lambda e: e.activation(out=Ysq, in_=Ys, func=AF.Square), reads=zk(10), writes=zk(11))
            op("dve", lambda e: e.tensor_reduce(out=STt[:, 8:16], in_=Ysq.rearrange("p (a b) -> p a b", a=8), axis=AX.X, op=ALU.add),
               reads=zk(11), writes=zk(4))
            mean, ex2, var_, rstd_ = STt[:, 16:24], STt[:, 24:32], STt[:, 32:40], STt[:, 40:48]
            op("dve", lambda e: e.tensor_scalar(out=mean, in0=STt[:, 0:8], scalar1=1.0 / 64, scalar2=None, op0=ALU.mult), reads=zk(4), writes=zk(4))
            op("dve", lambda e: e.tensor_scalar(out=ex2, in0=STt[:, 8:16], scalar1=1.0 / 64, scalar2=None, op0=ALU.mult), reads=zk(4), writes=zk(4))
            op("dve", lambda e: e.tensor_tensor(out=var_, in0=mean, in1=mean, op=ALU.mult), reads=zk(4), writes=zk(4))
            op("dve", lambda e: e.tensor_tensor(out=var_, in0=ex2, in1=var_, op=ALU.subtract), reads=zk(4), writes=zk(4))
            op("act", lambda e: e.activation(out=rstd_, in_=var_, func=AF.Sqrt, bias=64e-5, scale=1.0), reads=zk(4), writes=zk(4))
            op("dve", lambda e: e.reciprocal(out=rstd_, in_=rstd_), reads=zk(4), writes=zk(4))
            op("dve", lambda e: e.tensor_tensor(out=Y3, in0=Y3, in1=mean.unsqueeze(2).to_broadcast([128, 8, 64]), op=ALU.subtract),
               reads=zk(10) + zk(4), writes=zk(10))
            op("dve", lambda e: e.tensor_tensor(out=Y3, in0=Y3, in1=rstd_.unsqueeze(2).to_broadcast([128, 8, 64]), op=ALU.mult),
               reads=zk(10) + zk(4), writes=zk(10))
            op("pool", lambda e: e.tensor_tensor(out=Ys, in0=Ys, in1=bcv[:, 1024:1536], op=ALU.mult), reads=zk(10) + ["bcv"], writes=zk(10))
            op("pool", lambda e: e.tensor_tensor(out=Ys, in0=Ys, in1=bcv[:, 1536:2048], op=ALU.add), reads=zk(10) + ["bcv"], writes=zk(10))
            op("dve", lambda e: e.tensor_tensor(out=Ysq.rearrange("p (a b) -> p a b", a=8), in0=V_tm.rearrange("p (a b) -> p a b", a=8),
                                                in1=sm[:, 8:16].unsqueeze(2).to_broadcast([128, 8, 64]), op=ALU.mult),
               reads=zk(23) + ["sm8"], writes=zk(11))
            op("dve", lambda e: e.tensor_tensor(out=Ys, in0=Ys, in1=Ysq, op=ALU.add), reads=zk(10) + zk(11), writes=zk(10))
            for q in range(4):
                TR(B[4][:, q * 128:(q + 1) * 128], Ys[:, q * 128:(q + 1) * 128], ident[:], reads=zk(10) + ["ident"], writes=bk_(4))
            op("dve", lambda e: e.tensor_tensor(out=ycat[:, 0:4, :], in0=b4(4), in1=Gs, op=ALU.mult), reads=bk_(4) + zk(24), writes=zk(40))
            dmp("ycat_r", ycat[:, 0:4, :], zk(40), [128, 4, 128])

        def Dsa(g, mid=None):
            c = g % NCH
            c0 = c * 128
            SC = (c + 1) * 128
            dmp = lambda name, ap, keys, shape: dump(name, ap, keys, shape, g)
            sq, kx = Z[:, 29, :], Z[:, 30, :]
            Ga = z4(37)
            op("act", lambda e: e.activation(out=Ga, in_=pT[:, 23:27, 1:129], func=AF.Silu), reads=pk(23, 27), writes=zk(37))
            op("act", lambda e: e.activation(out=sq[:, 0:128], in_=pT[:, 20, 1:129], func=AF.Square), reads=pk(20, 21), writes=zk(29))
            MM(B[0][:, 0:128], ones[:], sq[:, 0:128], True, True, reads=["ones"] + zk(29), writes=bk_(0))
            op("act", lambda e: e.activation(out=sq[:, 128:256], in_=B[0][:, 0:128], func=AF.Sqrt, bias=1e-6, scale=1.0 / 128), reads=bk_(0), writes=zk(29))
            op("dve", lambda e: e.reciprocal(out=sq[:, 128:256], in_=sq[:, 128:256]), reads=zk(29), writes=zk(29))
            op("dve", lambda e: e.scalar_tensor_tensor(out=ckvT[:, c0:c0 + 128], in0=pT[:, 20, 1:129], scalar=cols[:, C_KVG:C_KVG + 1],
                                                       in1=sq[:, 128:256], op0=ALU.mult, op1=ALU.mult),
               reads=pk(20, 21) + ["cols"] + zk(29), writes=["ckvT%d" % c])
            TR(B[1][:, 0:128], ckvT[:, c0:c0 + 128], ident[:], reads=["ckvT%d" % c, "ident"], writes=bk_(1))
            op("act", lambda e: e.copy(out=ckv_tm[:, c, :], in_=B[1][:, 0:128]), reads=bk_(1), writes=["ckvtm%d" % c])
            MM(B[0][:, 128:256], bones[:], pT[:, 21, 1:129], True, True, reads=["bones"] + pk(21, 22), writes=bk_(0))
            op("dve", lambda e: e.scalar_tensor_tensor(out=kx[:, 0:128], in0=B[0][:, 128:256], scalar=-1.0 / 64, in1=pT[:, 21, 1:129],
                                                       op0=ALU.mult, op1=ALU.add), reads=pk(21, 22) + bk_(0), writes=zk(30))
            op("act", lambda e: e.activation(out=kx[:, 128:256], in_=kx[:, 0:128], func=AF.Square), reads=zk(30), writes=zk(30))
            MM(B[0][:, 256:384], bones[:], kx[:, 128:256], True, True, reads=["bones"] + zk(30), writes=bk_(0))
            op("act", lambda e: e.activation(out=kx[:, 256:384], in_=B[0][:, 256:384], func=AF.Sqrt, bias=1e-5, scale=1.0 / 64), reads=bk_(0), writes=zk(30))
            op("dve", lambda e: e.reciprocal(out=kx[:, 256:384], in_=kx[:, 256:384]), reads=zk(30), writes=zk(30))
            op("dve", lambda e: e.tensor_tensor(out=kx[:, 0:128], in0=kx[:, 0:128], in1=kx[:, 256:384], op=ALU.mult), reads=zk(30), writes=zk(30))
            op("dve", lambda e: e.tensor_scalar(out=kidxT[:, c0:c0 + 128], in0=kx[:, 0:128], scalar1=cols[:, C_KIG:C_KIG + 1],
                                                scalar2=cols[:, C_KIB:C_KIB + 1], op0=ALU.mult, op1=ALU.add),
               reads=zk(30) + ["cols"], writes=["kidxT%d" % c])
            cq = z4(31)
            op("act", lambda e: e.activation(out=cq[:, 0:2, :], in_=pT[:, 18:20, 1:129], func=AF.Square), reads=pk(18, 20), writes=zk(31))
            for kc in range(2):
                MM(B[0][:, 384:512], ones[:], cq[:, kc, :], (kc == 0), (kc == 1), reads=["ones"] + zk(31), writes=bk_(0))
            op("act", lambda e: e.activation(out=cq[:, 0, :], in_=B[0][:, 384:512], func=AF.Sqrt, bias=1e-6, scale=1.0 / 256), reads=bk_(0), writes=zk(31))
            op("dve", lambda e: e.reciprocal(out=cq[:, 0, :], in_=cq[:, 0, :]), reads=zk(31), writes=zk(31))
            op("dve", lambda e: e.tensor_tensor(out=cq[:, 2:4, :], in0=pT[:, 18:20, 1:129], in1=cq[:, 0:1, :].to_broadcast([128, 2, 128]), op=ALU.mult),
               reads=pk(18, 20) + zk(31), writes=zk(31))
            TR(B[1][:, 256:260], pT[0:4, 22, 1:129], ident[0:4, 0:4], reads=pk(22, 23) + ["ident"], writes=bk_(1))
            op("act", lambda e: e.mul(out=wi[:], in_=B[1][:, 256:260], mul=0.5), reads=bk_(1), writes=["wi"])
            for mt in range(4):
                for kc in range(2):
                    MM(B[2][:, mt * 128:(mt + 1) * 128], wuq[:, kc, mt * 128:(mt + 1) * 128], cq[:, 2 + kc, :], (kc == 0), (kc == 1),
                       reads=["wuq"] + zk(31), writes=bk_(2))
            qT = z4(32)
            op("act", lambda e: e.copy(out=qT, in_=b4(2)), reads=bk_(2), writes=zk(32))
            for mt in range(2):
                for kc in range(2):
                    MM(B[3][:, mt * 128:(mt + 1) * 128], wqi[:, kc, mt * 128:(mt + 1) * 128], cq[:, 2 + kc, :], (kc == 0), (kc == 1),
                       reads=["wqi"] + zk(31), writes=bk_(3))
            qiT = Z[:, 35, 0:256].rearrange("p (a b) -> p a b", a=2)
            op("act", lambda e: e.copy(out=qiT, in_=B[3][:, 0:256].rearrange("p (a b) -> p a b", a=2)), reads=bk_(3), writes=zk(35))
            qa = z2(33)
            for h in range(8):
                q, pb = h // 2, (h % 2) * 64
                bk = h // 4
                MM(B[bk][:, (h % 4) * 128:(h % 4 + 1) * 128], wuk[pb:pb + 64, q, :], qT[pb:pb + 64, q, :], True, True,
                   reads=["wuk"] + zk(32), writes=bk_(bk))
            for hf in range(2):
                op("act", lambda e, hf=hf: e.mul(out=qa[:, hf * 512:(hf + 1) * 512], in_=B[hf][:, :], mul=0.125), reads=bk_(hf), writes=zk(33 + hf))
            sc = Z[:, 25:29, :].rearrange("p a b -> p (a b)")
            SCK = zk(25, 29)
            rl = Z[:, 36, :]
            kxr = ["kidxT%d" % i for i in range(c + 1)]
            for s0 in range(0, SC, 512):
                w_ = min(512, SC - s0)
                P.burst_begin()
                for hi in range(4):
                    pb = (hi % 2) * 64
                    MM(B[hi][:, 0:w_], qiT[pb:pb + 64, hi // 2, :], kidxT[pb:pb + 64, s0:s0 + w_], True, True,
                       reads=zk(35) + kxr, writes=bk_(hi))
                P.burst_end()
                for hi in range(4):
                    bk = hi
                    op("act", lambda e, bk=bk, w_=w_: e.activation(out=rl[:, 0:w_], in_=B[bk][:, 0:w_], func=AF.Relu), reads=bk_(bk), writes=zk(36))
                    if hi == 0:
                        op("dve", lambda e, s0=s0, w_=w_: e.tensor_scalar(out=sc[:, s0:s0 + w_], in0=rl[:, 0:w_], scalar1=wi[:, 0:1], scalar2=None,
                                                                          op0=ALU.mult), reads=zk(36) + ["wi"], writes=SCK)
                    else:
                        op("dve", lambda e, s0=s0, w_=w_, hi=hi: e.scalar_tensor_tensor(out=sc[:, s0:s0 + w_], in0=rl[:, 0:w_], scalar=wi[:, hi:hi + 1],
                                                                                        in1=sc[:, s0:s0 + w_], op0=ALU.mult, op1=ALU.add),
                           reads=zk(36) + ["wi"] + SCK, writes=SCK)
            op("pool", lambda e: e.affine_select(out=sc[:, c0:c0 + 128], in_=sc[:, c0:c0 + 128], pattern=[[-1, 128]], compare_op=ALU.is_ge,
                                                 fill=NEG, base=0, channel_multiplier=1), reads=SCK, writes=SCK)
            dmp("score", sc[:, 0:SC], SCK, [128, SC])
            P.mark()
            if SC > TOPK:
                for rnd in range(TOPK // 8):
                    op("dve", lambda e: e.max(out=m8[:], in_=sc[:, 0:SC]), reads=SCK, writes=["m8"], w=(TKW * (SC / 512.0) ** TKP if TKW > 0 else 1.0), dur=0.15 + SC * 0.00105)
                    op("dve", lambda e: e.match_replace(out=sc[:, 0:SC], in_to_replace=m8[:], in_values=sc[:, 0:SC], imm_value=NEG),
                       reads=SCK + ["m8"], writes=SCK, w=(TKW * (SC / 512.0) ** TKP if TKW > 0 else 1.0), dur=0.15 + SC * 0.00105)
                op("dve", lambda e: e.tensor_scalar(out=sc[:, 0:SC], in0=sc[:, 0:SC], scalar1=-1.0e29, scalar2=None, op0=ALU.is_le),
                   reads=SCK, writes=SCK)
                op("dve", lambda e: e.tensor_tensor(out=sc[:, c0:c0 + 128], in0=sc[:, c0:c0 + 128], in1=mQS[:], op=ALU.mult),
                   reads=SCK + ["mQS"], writes=SCK)
            else:
                op("dve", lambda e: e.tensor_scalar(out=sc[:, 0:SC], in0=sc[:, 0:SC], scalar1=-1.0e29, scalar2=None, op0=ALU.is_ge),
                   reads=SCK, writes=SCK)
            dmp("mask", sc[:, 0:SC], SCK, [128, SC])
            P.mark()
            for kb0 in range(0, c + 1, 4):
                nb = min(4, c + 1 - kb0)
                for i in range(nb):
                    kb = kb0 + i
                    TR(B[0][:, i * 128:(i + 1) * 128], sc[:, kb * 128:(kb + 1) * 128], ident[:], reads=SCK + ["ident"], writes=bk_(0))
                op("act", lambda e, kb0=kb0, nb=nb: e.copy(out=mT[:, kb0:kb0 + nb, :], in_=B[0][:, 0:nb * 128].rearrange("p (a b) -> p a b", a=nb)),
                   reads=bk_(0), writes=["mT"])
            def pv_mm(kb):
                eb = 29 + 2 * (kb % 2)
                for hf in range(2):
                    MM(B[2 + hf][:, :], ckv_tm[:, kb, :], Z[:, eb + hf, :], (kb == 0), (kb == c),
                       reads=["ckvtm%d" % kb] + zk(eb + hf), writes=bk_(2 + hf))

            def pv_acc(kb):
                eb = 29 + 2 * (kb % 2)
                for hf in range(2):
                    if kb == 0:
                        op("pool", lambda e, hf=hf, eb=eb: e.tensor_copy(out=Z[:, 38 + hf, :], in_=Z[:, eb + hf, :]), reads=zk(eb + hf), writes=zk(38 + hf))
                    else:
                        op("pool", lambda e, hf=hf, eb=eb: e.tensor_tensor(out=Z[:, 38 + hf, :], in0=Z[:, 38 + hf, :], in1=Z[:, eb + hf, :], op=ALU.add),
                           reads=zk(eb + hf) + zk(38 + hf), writes=zk(38 + hf))

            for kb in range(c + 1):
                eb = 29 + 2 * (kb % 2)
                P.burst_begin()
                for hf in range(2):
                    MM(B[hf][:, :], ckvT[:, kb * 128:(kb + 1) * 128], qa[:, hf * 512:(hf + 1) * 512], True, True,
                       reads=["ckvT%d" % kb] + zk(33, 35), writes=bk_(hf))
                if kb > 0:
                    pv_mm(kb - 1)
                P.burst_end()
                if kb > 0:
                    pv_acc(kb - 1)
                for hf in range(2):
                    op("act", lambda e, hf=hf, eb=eb: e.activation(out=Z[:, eb + hf, :], in_=B[hf][:, :], func=AF.Exp), reads=bk_(hf), writes=zk(eb + hf))
                    op(PM_ENG, lambda e, hf=hf, eb=eb, kb=kb: e.tensor_tensor(out=z4(eb + hf), in0=z4(eb + hf),
                                                                             in1=mT[:, kb:kb + 1, :].to_broadcast([128, 4, 128]), op=ALU.mult),
                       reads=zk(eb + hf) + ["mT"], writes=zk(eb + hf))
            pv_mm(c)
            pv_acc(c)
            ol = Z[:, 35:37, :]
            for hf in range(2):
                MM(B[hf][:, :], ones[:], Z[:, 38 + hf, :], True, True, reads=["ones"] + zk(38 + hf), writes=bk_(hf))
                op("dve", lambda e, hf=hf: e.reciprocal(out=Z[:, 38 + hf, :], in_=B[hf][:, :]), reads=bk_(hf), writes=zk(38 + hf))
                op("dve", lambda e, hf=hf: e.tensor_tensor(out=ol[:, hf, :], in0=B[2 + hf][:, :], in1=Z[:, 38 + hf, :], op=ALU.mult),
                   reads=bk_(2 + hf) + zk(38 + hf), writes=zk(35 + hf))
            dmp("olatT", ol, zk(35, 37), [128, 2, 512])
            for h in range(8):
                MM(B[0][:, h * 64:(h + 1) * 64], ol[:, h // 4, (h % 4) * 128:(h % 4 + 1) * 128], wuv[:, h, :], True, True,
                   reads=zk(35, 37) + ["wuv"], writes=bk_(0))
            otm = Z[:, 32, :]
            op("act", lambda e: e.copy(out=otm, in_=B[0][:, :]), reads=bk_(0), writes=zk(32))
            for q in range(4):
                TR(B[1][:, q * 128:(q + 1) * 128], otm[:, q * 128:(q + 1) * 128], ident[:], reads=zk(32) + ["ident"], writes=bk_(1))
            op("dve", lambda e: e.tensor_tensor(out=ycat[:, 4:8, :], in0=b4(1), in1=Ga, op=ALU.mult), reads=bk_(1) + zk(37), writes=zk(41))
            dmp("ycat", ycat[:], YCK, [128, 8, 128])
            op("sp", lambda e: e.dma_start(out=z2(31), in_=x_d[g * 128:(g + 1) * 128, :]), writes=zk(31, 33), dma="dma9")
            for kt in range(NWO):
                slot = (wo_ctr[0] + kt) % NWO
                op("sp", lambda e, kt=kt, slot=slot: e.dma_start(out=wosl[slot][:], in_=wout_d[:, kt, :]), writes=["wosl%d" % slot],
                   dma="dma%d" % (6 + slot))

        def O(g):
            row0 = g * 128
            xre = z2(31)
            for kt in range(8):
                slot = wo_ctr[0] % NWO
                wo_ctr[0] += 1
                if kt >= NWO:
                    op("sp", lambda e, kt=kt, slot=slot: e.dma_start(out=wosl[slot][:], in_=wout_d[:, kt, :]), writes=["wosl%d" % slot],
                       dma="dma%d" % (6 + slot))
                P.burst_begin()
                for hf in range(2):
                    MM(B[hf][:, :], ycat[:, kt, :], wosl[slot][:, hf * 512:(hf + 1) * 512], (kt == 0), (kt == 7),
                       reads=YCK + ["wosl%d" % slot], writes=bk_(hf))
                P.burst_end()
            res = Z[:, 29:31, :]
            for hf in range(2):
                op("dve", lambda e, hf=hf: e.tensor_tensor(out=res[:, hf, :], in0=B[hf][:, :], in1=xre[:, hf * 512:(hf + 1) * 512], op=ALU.add),
                   reads=bk_(hf) + zk(31, 33), writes=zk(29 + hf))
            res2 = z2(29)
            junk = z2(31)
            op("act", lambda e: e.activation(out=junk, in_=res2, func=AF.Square, accum_out=sm[:, 3:4]), reads=zk(29, 31), writes=zk(31, 33) + ["sm3"])
            op("act", lambda e: e.activation(out=sm[:, 4:5], in_=sm[:, 3:4], func=AF.Sqrt, bias=1e-6, scale=1.0 / 1024), reads=["sm3"], writes=["sm4"])
            op("dve", lambda e: e.reciprocal(out=sm[:, 5:6], in_=sm[:, 4:5]), reads=["sm4"], writes=["sm5"])
            op("dve", lambda e: e.scalar_tensor_tensor(out=junk, in0=res2, scalar=sm[:, 5:6], in1=bcv[:, 0:1024], op0=ALU.mult, op1=ALU.mult),
               reads=zk(29, 31) + ["sm5", "bcv"], writes=zk(31, 33))
            op("act", lambda e: e.dma_start(out=out_d[row0:row0 + 128, :], in_=junk), reads=zk(31, 33), dma="dma10", final=True)

        LAG = 0.0
        XP(0)
        chunks = []
        for g in range(NG):
            def streamA():
                def midA():
                    if g + 1 < NG:
                        XP(g + 1, bt=(6, 7), ba=(4, 5), zs=5)
                R(g, midA)

            def streamB():
                if g > 0:
                    O(g - 1)
                Dsa(g, None)
            chunks.append((P.record(streamA), P.record(streamB)))
        P.merge_global(chunks, LAG)
        O(NG - 1)
        P.finish()
    return nc, dbg_d


def prep_weights(inp):
    f = lambda a: np.ascontiguousarray(np.asarray(a, np.float32))
    w_in = f(inp["w_in"])[0]
    colmap = [(0, 128, False), (128, 128, False), (256, 128, False), (384, 128, False),
              (512, 128, False), (640, 128, False), (768, 128, False), (896, 128, False),
              (1024, 128, False), (1152, 128, False), (1280, 128, False), (1408, 128, False),
              (1536, 32, False), (1568, 32, False),
              (1600, 128, False), (1728, 128, False), (1856, 128, False), (1984, 128, False),
              (2112, 128, False), (2240, 128, False), (2368, 128, False),
              (2496, 64, True), (2560, 4, False),
              (2564, 128, False), (2692, 128, False), (2820, 128, False), (2948, 128, False)]
    w_in_r = np.zeros((NCT, 128, 8, 128), np.float32)
    wk = w_in.reshape(8, 128, 3076)
    for ct, (c0, n, dup) in enumerate(colmap):
        blk = wk[:, :, c0:c0 + n].transpose(1, 0, 2)
        w_in_r[ct, :, :, 0:n] = blk
        if dup:
            w_in_r[ct, :, :, 64:64 + n] = blk
    cols = np.zeros((128, NCOLS), np.float32)
    mu = f(inp["mu_shift"])[0]
    cols[:, C_MU:C_MU + 12] = mu[0:1536].reshape(12, 128).T
    cols[0:32, C_MU + 12] = mu[1536:1568]
    cols[0:32, C_MU + 13] = mu[1568:1600]
    cols[:, C_W0:C_W0 + 4] = f(inp["w0"])[0].reshape(4, 128).T
    cols[:, C_A0:C_A0 + 4] = f(inp["a0"])[0].reshape(4, 128).T
    cols[:, C_KK:C_KK + 4] = f(inp["k_k"])[0].reshape(4, 128).T
    cols[:, C_KA:C_KA + 4] = f(inp["k_a"])[0].reshape(4, 128).T
    cols[:, C_RK:C_RK + 4] = f(inp["r_k"])[0].reshape(4, 128).T
    cols[:, C_NG:C_NG + 8] = f(inp["norm_g"])[0].reshape(8, 128).T
    cols[:, C_QG:C_QG + 2] = f(inp["q_norm_g"])[0].reshape(2, 128).T
    cols[:, C_KVG] = f(inp["kv_norm_g"])[0]
    cols[:, C_KIG] = np.tile(f(inp["kidx_g"])[0], 2)
    cols[:, C_KIB] = np.tile(f(inp["kidx_b"])[0], 2)
    bcv = np.zeros((128, 2048), np.float32)
    bcv[:, 0:1024] = f(inp["final_g"])[None, :]
    bcv[:, 1024:1536] = f(inp["gn_g"])[0].reshape(1, 512)
    bcv[:, 1536:2048] = f(inp["gn_b"])[0].reshape(1, 512)
    w_uk = f(inp["w_uk"])[0]
    w_ukT_r = np.zeros((128, 4, 128), np.float32)
    for h in range(8):
        w_ukT_r[(h % 2) * 64:(h % 2) * 64 + 64, h // 2, :] = w_uk[h].T
    return {
        "w_in_r": w_in_r,
        "cols": cols,
        "bcv": bcv,
        "w_up": f(inp["w_up"])[0],
        "a_up": f(inp["a_up"])[0],
        "w_uq_r": f(f(inp["w_uq"])[0].reshape(2, 128, 512).transpose(1, 0, 2)),
        "w_qidx_r": f(f(inp["w_qidx"])[0].reshape(2, 128, 256).transpose(1, 0, 2)),
        "w_ukT_r": w_ukT_r,
        "w_uv_r": f(f(inp["w_uv"])[0].transpose(1, 0, 2)),
        "w_out_r": f(f(inp["w_out"])[0].reshape(8, 128, 1024).transpose(1, 0, 2)),
    }


def kernel(**inputs):
    x = np.asarray(inputs["x"], np.float32)
    Bsz, T, Dm = x.shape
    n = 8
    NSEQ = Bsz // n
    wts = prep_weights(inputs)
    nc, _ = build(NSEQ, T // 128)
    in_maps = []
    for i in range(n):
        m = dict(wts)
        m["x"] = np.ascontiguousarray(x[i * NSEQ:(i + 1) * NSEQ].reshape(NSEQ * T, Dm))
        in_maps.append(m)
    res = run_bass_kernel_spmd(nc, in_maps, core_ids=list(range(n)))
    out = np.concatenate([np.asarray(r["out"], np.float32).reshape(NSEQ, T, Dm) for r in res.results], axis=0)
    return out
```

```python
from contextlib import ExitStack
import math
import numpy as np
import concourse.bass as bass
import concourse.mybir as mybir
from concourse.bass_utils import run_bass_kernel_spmd

F32 = mybir.dt.float32
BF16 = mybir.dt.bfloat16
ALU = mybir.AluOpType
AF = mybir.ActivationFunctionType
AX = mybir.AxisListType

ENGS = ("pe", "act", "dve", "pool", "sp")
KAPPA = math.exp(-0.5)
NEG = -1.0e30
NCT = 27
TOPK = 256
PE_MODE_DRAIN = False
USE_MARKS = False
MERGE_MODE = "prop"
NO_SELF_SYNC = ()

C_MU, C_W0, C_A0, C_KK, C_KA, C_RK, C_NG, C_QG, C_KVG, C_KIG, C_KIB, NCOLS = 0, 14, 18, 22, 26, 30, 34, 42, 44, 45, 46, 47


class Prog:
    def __init__(self, nc, es, n_dma_sems=12):
        self.nc = nc
        self.es = es
        self.ops = {e: [] for e in ENGS}
        self.cnt = {e: 0 for e in ENGS}
        self.sems = {}
        for e in ENGS:
            self.sems[e] = es.enter_context(nc.semaphore("s_" + e))
        for i in range(n_dma_sems):
            k = "dma%d" % i
            self.sems[k] = es.enter_context(nc.semaphore("s_" + k))
            self.cnt[k] = 0
        self.seen = {e: {} for e in ENGS}
        self.last_w = {}
        self.readers = {}
        self.pe_mode = None
        self.pe_group = False
        self.pe_stream = -1
        self.cur_stream = -1
        self.rec = None
        self.final_toks = []
        self.burst_stack = []

    def sb(self, name, shape, dt=F32):
        return self.es.enter_context(self.nc.sbuf_tensor("sb_" + name, list(shape), dt))

    def ps(self, name, shape, dt=F32):
        return self.es.enter_context(self.nc.psum_tensor("ps_" + name, list(shape), dt))

    def record(self, fn):
        assert self.rec is None
        self.rec = []
        fn()
        r, self.rec = self.rec, None
        return r

    def burst_begin(self):
        if self.rec is not None:
            self.burst_stack.append(self.rec)
            self.rec = []

    def burst_end(self):
        if self.rec is not None and self.burst_stack:
            inner = self.rec
            self.rec = self.burst_stack.pop()
            if inner:
                tot = sum(o[1].get("w", 1.0) for o in inner)
                first = inner[0]
                kw = dict(first[1])
                kw["w"] = tot
                kw["burst"] = inner
                self.rec.append((first[0], kw))

    def mark(self):
        if self.rec is not None and USE_MARKS:
            self.rec.append("MARK")

    def merge(self, a, b):
        def split(x):
            segs = [[]]
            for o in x:
                if o == "MARK":
                    segs.append([])
                else:
                    segs[-1].append(o)
            return segs
        sa, sb = split(a), split(b)
        while len(sa) < len(sb):
            sa.append([])
        while len(sb) < len(sa):
            sb.append([])
        for xa, xb in zip(sa, sb):
            self._merge_seg(xa, xb)

    def merge_global(self, chunks, lag):
        A, Bq = [], []
        for g, (a, b) in enumerate(chunks):
            for x, dst, off in ((a, A, 0.0), (b, Bq, lag)):
                tot = sum(o[1].get("w", 1.0) for o in x) or 1.0
                acc = 0.0
                for o in x:
                    w = o[1].get("w", 1.0)
                    dst.append((g + off + (acc + 0.5 * w) / tot, o))
                    acc += w
        wr_a = set(k for _, o in A for k in o[1]["writes"])
        rd_b = set(k for _, o in Bq for k in o[1]["reads"])
        shared = wr_a & rd_b
        ia = ib = 0
        rd_pos = {}
        for j, (kk, o) in enumerate(Bq):
            for k in o[1]["reads"]:
                if k in shared:
                    rd_pos.setdefault(k, []).append((int(kk - lag + 1e-9), j))
        while ia < len(A) or ib < len(Bq):
            if ia >= len(A):
                take_b = True
            elif ib >= len(Bq):
                take_b = False
            else:
                ka, oa = A[ia]
                need = -1
                for k in oa[1]["writes"]:
                    for (gb, j) in rd_pos.get(k, ()):
                        if gb <= int(ka) and j >= ib:
                            need = max(need, j)
                take_b = need >= ib or Bq[ib][0] < ka
            _, o = (Bq[ib] if take_b else A[ia])
            if take_b:
                ib += 1
            else:
                ia += 1
            self.cur_stream = 1 if take_b else 0
            self.op(*o[0], **o[1])
        self.cur_stream = -1

    def _merge_seg(self, a, b):
        if not a or not b:
            for o in a + b:
                self.op(*o[0], **o[1])
            return
        if MERGE_MODE == "sim":
            return self._merge_sim(a, b)
        def keyed(x, tag):
            tot = sum(o[1].get("w", 1.0) for o in x)
            acc, out = 0.0, []
            for i, o in enumerate(x):
                w = o[1].get("w", 1.0)
                out.append(((acc + 0.5 * w) / tot, tag, i, o))
                acc += w
            return out
        items = keyed(a, 0) + keyed(b, 1)
        items.sort(key=lambda t: (t[0], t[1], t[2]))
        for _, _, _, o in items:
            self.op(*o[0], **o[1])

    def _merge_sim(self, a, b):
        DEF = {"pe": 0.6, "act": 0.5, "dve": 0.5, "pool": 0.9, "sp": 0.15}
        wr_a = set(k for o in a for k in o[1]["writes"])
        rd_b = set(k for o in b for k in o[1]["reads"])
        wr_b = set(k for o in b for k in o[1]["writes"])
        rd_a = set(k for o in a for k in o[1]["reads"])
        shared = wr_a & rd_b
        assert not ((wr_b & rd_a) - shared), sorted((wr_b & rd_a) - shared)
        last_rd_b = {}
        for j, o in enumerate(b):
            for k in o[1]["reads"]:
                if k in shared:
                    last_rd_b[k] = j
        eng_free, ready_w, ready_r = {}, {}, {}
        rem = [sum(o[1].get("dur") or DEF[o[0][0]] for o in a), sum(o[1].get("dur") or DEF[o[0][0]] for o in b)]
        ptr = [0, 0]
        streams = [a, b]

        def est(o):
            (eng, _), kw = o
            t = eng_free.get(eng, 0.0)
            for k in kw["reads"]:
                t = max(t, ready_w.get(k, 0.0))
            for k in kw["writes"]:
                t = max(t, ready_w.get(k, 0.0), ready_r.get(k, 0.0))
            return t

        def emit(si):
            o = streams[si][ptr[si]]
            ptr[si] += 1
            (eng, _), kw = o
            d = kw.get("dur") or DEF[eng]
            st = est(o)
            if kw["dma"] is not None:
                eng_free[eng] = st + 0.15
                done = st + d
            else:
                eng_free[eng] = st + d
                done = st + d + 0.25
            for k in kw["reads"]:
                ready_r[k] = max(ready_r.get(k, 0.0), done)
            for k in kw["writes"]:
                ready_w[k] = done
            rem[si] -= d
            kw2 = dict(kw)
            kw2.pop("dur", None)
            self.op(*o[0], **kw2)

        while ptr[0] < len(a) or ptr[1] < len(b):
            if ptr[0] >= len(a):
                emit(1)
                continue
            if ptr[1] >= len(b):
                emit(0)
                continue
            oa = a[ptr[0]]
            need = max([last_rd_b.get(k, -1) for k in oa[1]["writes"]] + [-1])
            if need >= ptr[1]:
                emit(1)
                continue
            ta, tb = est(oa), est(b[ptr[1]])
            if ta < tb or (ta == tb and rem[0] >= rem[1]):
                emit(0)
            else:
                emit(1)

    def op(self, eng, fn, reads=(), writes=(), dma=None, mode=None, accum=False, final=False, w=1.0, dur=None, group=False, burst=None):
        if burst is not None:
            for o in burst:
                self.op(*o[0], **o[1])
            return None
        if self.rec is not None:
            self.rec.append(((eng, fn), dict(reads=list(reads), writes=list(writes), dma=dma, mode=mode, accum=accum, final=final, w=w, dur=dur, group=group)))
            return None
        deps = {}

        def add(tok):
            s, v = tok
            if deps.get(s, 0) < v:
                deps[s] = v
        for k in reads:
            if k in self.last_w:
                add(self.last_w[k])
        for k in writes:
            if k in self.last_w:
                add(self.last_w[k])
            for t in self.readers.get(k, ()):
                add(t)
        waits = []
        if eng == "pe":
            if mode != self.pe_mode and self.cnt["pe"] > 0 and (PE_MODE_DRAIN or ((group or self.pe_group) and self.cur_stream != self.pe_stream)):
                deps["pe"] = self.cnt["pe"]
            elif "pe" in deps and accum:
                del deps["pe"]
            self.pe_mode = mode
            self.pe_group = group
            self.pe_stream = self.cur_stream
        for s, v in deps.items():
            if s == eng and eng in NO_SELF_SYNC:
                continue
            if self.seen[eng].get(s, 0) >= v:
                continue
            self.seen[eng][s] = v
            waits.append((s, v))
        if dma is None:
            self.cnt[eng] += 1
            tok = (eng, self.cnt[eng])
            inc = 1
        else:
            self.cnt[dma] += 16
            tok = (dma, self.cnt[dma])
            inc = 16
        self.ops[eng].append((waits, fn, tok[0], inc))
        for k in reads:
            self.readers.setdefault(k, []).append(tok)
        for k in writes:
            self.last_w[k] = tok
            self.readers[k] = []
        if final:
            self.final_toks.append(tok)
        return tok

    def finish(self):
        nc = self.nc
        fin = {}
        for s, v in self.final_toks:
            fin[s] = max(fin.get(s, 0), v)
        with nc.Block() as block:
            def replay(e, engname):
                for waits, fn, semk, inc in self.ops[engname]:
                    for s, v in waits:
                        e.wait_ge(self.sems[s], v)
                    ins = fn(e)
                    ins.then_inc(self.sems[semk], inc)

            @block.tensor
            def _(e):
                replay(e, "pe")

            @block.scalar
            def _(e):
                replay(e, "act")

            @block.vector
            def _(e):
                replay(e, "dve")

            @block.gpsimd
            def _(e):
                replay(e, "pool")

            @block.sync
            def _(e):
                replay(e, "sp")
                for s, v in fin.items():
                    e.wait_ge(self.sems[s], v)


def build(NSEQ, NCH, debug=None):
    debug = debug or {}
    T = NCH * 128
    NG = NSEQ * NCH
    nc = bass.Bass("TRN2", target_bir_lowering=False)
    D_ = lambda name, shape: nc.dram_tensor(name, list(shape), F32, kind="ExternalInput").ap()
    x_d = D_("x", [NSEQ * T, 1024])
    win_d = D_("w_in_r", [NCT, 128, 8, 128])
    cols_d = D_("cols", [128, NCOLS])
    bcv_d = D_("bcv", [128, 2048])
    wup_d = D_("w_up", [32, 512])
    aup_d = D_("a_up", [32, 512])
    wuq_d = D_("w_uq_r", [128, 2, 512])
    wqi_d = D_("w_qidx_r", [128, 2, 256])
    wuk_d = D_("w_ukT_r", [128, 4, 128])
    wuv_d = D_("w_uv_r", [128, 8, 64])
    wout_d = D_("w_out_r", [128, 8, 1024])
    out_d = nc.dram_tensor("out", [NSEQ * T, 1024], F32, kind="ExternalOutput").ap()
    dbg_d = {}

    with ExitStack() as es:
        P = Prog(nc, es)
        op = P.op
        ident = P.sb("ident", [128, 128])
        ones = P.sb("ones", [128, 128])
        bones = P.sb("bones", [128, 128])
        hsel = P.sb("hsel", [128, 2])
        mSI2 = P.sb("mSI2", [128, 512])
        mSL = P.sb("mSL", [128, 128])
        mQS = P.sb("mQS", [128, 128])
        cols = P.sb("cols", [128, NCOLS])
        omka = P.sb("omka", [128, 4])
        bcv = P.sb("bcv", [128, 2048])
        wup_t = P.sb("wup", [32, 512])
        aup_t = P.sb("aup", [32, 512])
        wup, aup = wup_t[:], aup_t[:]
        wuq = P.sb("wuq", [128, 2, 512])
        wqi = P.sb("wqi", [128, 2, 256])
        wuk = P.sb("wuk", [128, 4, 128])
        wuv = P.sb("wuv", [128, 8, 64])
        NWO = 3
        wosl = [P.sb("wosl%d" % i, [128, 1024]) for i in range(NWO)]
        xT = P.sb("xT", [128, 8, 128])
        NSLOT = 3
        wsl = [P.sb("wsl%d" % i, [128, 8, 128]) for i in range(NSLOT)]
        pT = P.sb("pT", [128, NCT, 129])
        sm = P.sb("sm", [128, 16])
        S = P.sb("S", [128, 4, 64])
        S0m = P.sb("S0m", [128, 4, 64])
        gm = P.sb("gm", [128, 4])
        gC = P.sb("gC", [128, 4])
        c63 = P.sb("c63", [128, 4])
        NA = [P.sb("NA%d" % h, [128, 384]) for h in range(8)]
        TT = [P.sb("TT%d" % h, [128, 128]) for h in range(8)]
        AAk = [[P.sb("AA%d_%d" % (s, q), [128, 256]) for q in range(2)] for s in range(4)]
        kidxT = P.sb("kidxT", [128, T])
        ckvT = P.sb("ckvT", [128, T])
        ckv_tm = P.sb("ckv_tm", [128, NCH, 128])
        mT = P.sb("mT", [128, 16, 128], BF16)
        m8 = P.sb("m8", [128, 8])
        wi = P.sb("wi", [128, 4])
        NZ = 42
        Z = P.sb("Z", [128, NZ, 512])
        zk = lambda a, b=None: ["Z%d" % i for i in range(a, (a + 1) if b is None else b)]
        ycat = Z[:, 40:42, :].rearrange("p a b -> p (a b)").rearrange("p (a b) -> p a b", a=8)
        YCK = zk(40, 42)

        def z4(i):
            return Z[:, i, :].rearrange("p (a b) -> p a b", a=4)

        def z2(i):
            return Z[:, i:i + 2, :].rearrange("p a b -> p (a b)")
        B = [P.ps("B%d" % i, [128, 512]) for i in range(8)]
        b4 = lambda i: B[i][:, :].rearrange("p (a b) -> p a b", a=4)
        bk_ = lambda i: ["B%da" % i, "B%db" % i]

        ld = []
        for dst, src, key in [(cols[:], cols_d, "cols"), (bcv[:], bcv_d, "bcv"), (wup, wup_d, "wup"), (aup, aup_d, "aup"),
                              (wuq[:], wuq_d, "wuq"), (wqi[:], wqi_d, "wqi"), (wuk[:], wuk_d, "wuk"), (wuv[:], wuv_d, "wuv")]:
            ld.append((key, op("sp", lambda e, dst=dst, src=src: e.dma_start(out=dst, in_=src), writes=[key], dma="dma0")))
        for key, _ in ld:
            P.last_w[key] = ld[-1][1]
        op("pool", lambda e: e.memset(ones[:], 1.0), writes=["ones"])
        op("pool", lambda e: e.affine_select(out=ident[:], in_=ones[:], pattern=[[-1, 128]], compare_op=ALU.is_equal,
                                             fill=0.0, base=0, channel_multiplier=1), reads=["ones"], writes=["ident"])
        for q in range(4):
            cmp_ = ALU.is_gt if q % 2 == 0 else ALU.is_ge
            op("pool", lambda e, q=q, cmp_=cmp_: e.affine_select(out=mSI2[:, q * 128:(q + 1) * 128], in_=ones[:], pattern=[[1, 128]],
                                                                 compare_op=cmp_, fill=0.0, base=0, channel_multiplier=-1),
               reads=["ones"], writes=["mSI2"])
        op("pool", lambda e: e.affine_select(out=mSL[:], in_=ones[:], pattern=[[-1, 128]], compare_op=ALU.is_gt,
                                             fill=0.0, base=0, channel_multiplier=1), reads=["ones"], writes=["mSL"])
        op("pool", lambda e: e.affine_select(out=mQS[:], in_=ones[:], pattern=[[-1, 128]], compare_op=ALU.is_ge,
                                             fill=0.0, base=0, channel_multiplier=1), reads=["ones"], writes=["mQS"])
        op("pool", lambda e: e.memset(bones[:], 0.0), writes=["bones"])
        op("pool", lambda e: e.memset(bones[0:64, 0:64], 1.0), writes=["bones"])
        op("pool", lambda e: e.memset(bones[64:128, 64:128], 1.0), writes=["bones"])
        op("pool", lambda e: e.memset(hsel[:], 0.0), writes=["hsel"])
        op("pool", lambda e: e.memset(hsel[0:64, 0:1], 1.0), writes=["hsel"])
        op("pool", lambda e: e.memset(hsel[64:128, 1:2], 1.0), writes=["hsel"])
        op("dve", lambda e: e.tensor_scalar(out=omka[:], in0=cols[:, C_KA:C_KA + 4], scalar1=-1.0, scalar2=1.0,
                                            op0=ALU.mult, op1=ALU.add), reads=["cols"], writes=["omka"])
        for kc in range(2):
            op("dve", lambda e, kc=kc: e.tensor_scalar(out=wuq[:, kc, :], in0=wuq[:, kc, :], scalar1=cols[:, C_QG + kc:C_QG + kc + 1],
                                                       scalar2=None, op0=ALU.mult), reads=["cols", "wuq"], writes=["wuq"])
            op("dve", lambda e, kc=kc: e.tensor_scalar(out=wqi[:, kc, :], in0=wqi[:, kc, :], scalar1=cols[:, C_QG + kc:C_QG + kc + 1],
                                                       scalar2=None, op0=ALU.mult), reads=["cols", "wqi"], writes=["wqi"])

        def dump(name, ap, keys, shape, g):
            if name in debug and (debug[name][0] * NCH + debug[name][1]) == g:
                d = nc.dram_tensor("dbg_" + name, list(shape), F32, kind="ExternalOutput").ap()
                dbg_d[name] = d
                op("sp", lambda e: e.dma_start(out=d, in_=ap), reads=keys, dma="dma11", final=True)

        TKW = 3.0
        TKP = 1.0
        PM_ENG = "dve"
        wtile_ctr = [0]
        wo_ctr = [0]
        rup = lambda n: 32 if n <= 32 else 64 if n <= 64 else 128

        def MM(out, lhsT, rhs, start, stop, reads, writes):
            mode = ("M", rup(lhsT.shape[0]), rup(lhsT.shape[-1]))
            return op("pe", lambda e: e.matmul(out, lhsT=lhsT, rhs=rhs, start=start, stop=stop), reads=reads, writes=writes,
                      mode=mode, accum=(not start), dur=0.3 + 0.0034 * rhs.shape[-1], group=not (start and stop))

        def TR(out, in_, identity, reads, writes):
            mode = ("T", rup(in_.shape[0]), rup(in_.shape[-1]))
            return op("pe", lambda e: e.transpose(out=out, in_=in_, identity=identity), reads=reads, writes=writes, mode=mode, dur=0.35)

        pk = lambda a, b: ["pT%d" % i for i in range(a, b)]

        def XP(g, bt=(2, 3), ba=(0, 1), zs=29, t0=0, t1=NCT):
            row0 = g * 128
            if t0 == 0:
                XP_head(g, bt, zs)
            for ct in range(t0, t1):
                slot = wtile_ctr[0] % NSLOT
                wtile_ctr[0] += 1
                op("sp", lambda e, ct=ct, slot=slot: e.dma_start(out=wsl[slot][:], in_=win_d[ct]), writes=["wsl%d" % slot],
                   dma="dma%d" % (2 + slot), dur=3.7)
                gi, r = (ct - t0) // 4, (ct - t0) % 4
                bk = ba[gi % 2]
                P.burst_begin()
                for kc in range(8):
                    MM(B[bk][:, r * 128:(r + 1) * 128], wsl[slot][:, kc, :], xT[:, kc, :], (kc == 0), (kc == 7),
                       reads=["wsl%d" % slot, "xT"], writes=bk_(bk))
                P.burst_end()
                if r == 3 or ct == t1 - 1:
                    c0_, n_ = ct - r, r + 1
                    op("act", lambda e, c0_=c0_, n_=n_, bk=bk: e.copy(out=pT[:, c0_:c0_ + n_, 1:129],
                                                                      in_=B[bk][:, 0:n_ * 128].rearrange("p (a b) -> p a b", a=n_)),
                       reads=bk_(bk), writes=["pT%d" % i for i in range(c0_, c0_ + n_)])

        def XP_head(g, bt, zs):
            row0 = g * 128
            xcb = z2(zs + 2)
            xkk = zk(zs + 2, zs + 4)
            op("sp", lambda e: e.dma_start(out=xcb, in_=x_d[row0:row0 + 128, :]), writes=xkk, dma="dma1")
            xs2 = z2(zs)
            zsk = zk(zs, zs + 2)
            op("act", lambda e: e.activation(out=xs2, in_=xcb, func=AF.Square, accum_out=sm[:, 0:1]), reads=xkk, writes=zsk + ["sm0"])
            op("act", lambda e: e.activation(out=sm[:, 1:2], in_=sm[:, 0:1], func=AF.Sqrt, bias=1e-6, scale=1.0 / 1024),
               reads=["sm0"], writes=["sm1"])
            op("dve", lambda e: e.reciprocal(out=sm[:, 2:3], in_=sm[:, 1:2]), reads=["sm1"], writes=["sm2"])
            op("dve", lambda e: e.tensor_scalar(out=xs2, in0=xcb, scalar1=sm[:, 2:3], scalar2=None, op0=ALU.mult),
               reads=xkk + ["sm2"], writes=zsk)
            for hf in range(2):
                for q in range(4):
                    kc = hf * 4 + q
                    TR(B[bt[hf]][:, q * 128:(q + 1) * 128], xs2[:, kc * 128:(kc + 1) * 128], ident[:],
                       reads=zsk + ["ident"], writes=bk_(bt[hf]))
                op("dve", lambda e, hf=hf: e.tensor_tensor(out=xT[:, hf * 4:(hf + 1) * 4, :], in0=b4(bt[hf]),
                                                           in1=cols[:, C_NG + hf * 4:C_NG + hf * 4 + 4].unsqueeze(2).to_broadcast([128, 4, 128]),
                                                           op=ALU.mult), reads=bk_(bt[hf]) + ["cols"], writes=["xT"])

        def R(g, mid=None):
            c = g % NCH
            dmp = lambda name, ap, keys, shape: dump(name, ap, keys, shape, g)
            if c == 0:
                op("dve", lambda e: e.memset(S[:], 0.0), writes=["S"])
                op("dve", lambda e: e.memset(pT[:, 0:14, 0:1], 0.0), writes=pk(0, 14))
            dmp("pT", pT[:], pk(0, NCT), [128, NCT, 129])
            XS = Z[:, 0:4, :].rearrange("p a b -> p (a b)")[:, 0:14 * 128].rearrange("p (a b) -> p a b", a=14)
            XSk = zk(0, 4)
            mu_bc = cols[:, C_MU:C_MU + 14].unsqueeze(2).to_broadcast([128, 14, 128])
            op("dve", lambda e: e.tensor_tensor(out=XS, in0=pT[:, 0:14, 0:128], in1=pT[:, 0:14, 1:129], op=ALU.subtract),
               reads=pk(0, 14), writes=XSk)
            op("dve", lambda e: e.tensor_tensor(out=XS, in0=XS, in1=mu_bc, op=ALU.mult), reads=XSk + ["cols"], writes=XSk)
            op("dve", lambda e: e.tensor_tensor(out=XS, in0=XS, in1=pT[:, 0:14, 1:129], op=ALU.add), reads=XSk + pk(0, 14), writes=XSk)
            op("dve", lambda e: e.tensor_copy(out=pT[:, 0:14, 0:1], in_=pT[:, 0:14, 128:129]), reads=pk(0, 14), writes=pk(0, 14))
            Gs = z4(24)
            op("act", lambda e: e.activation(out=Gs, in_=pT[:, 14:18, 1:129], func=AF.Silu), reads=pk(14, 18), writes=zk(24))
            xr, xk_, xv = XS[:, 0:4, :], XS[:, 4:8, :], XS[:, 8:12, :]
            op("act", lambda e: e.activation(out=Z[0:32, 4, 0:128], in_=XS[0:32, 12, :], func=AF.Tanh), reads=XSk, writes=zk(4))
            for q in range(4):
                MM(B[4][:, q * 128:(q + 1) * 128], wup[:, q * 128:(q + 1) * 128], Z[0:32, 4, 0:128], True, True,
                   reads=["wup"] + zk(4), writes=bk_(4))
            for q in range(4):
                MM(B[5][:, q * 128:(q + 1) * 128], aup[:, q * 128:(q + 1) * 128], XS[0:32, 13, :], True, True,
                   reads=["aup"] + XSk, writes=bk_(5))
            sg, a_t, cs, E1, Einv, Ehat, kk, T1, bb = (z4(i) for i in (5, 6, 7, 8, 9, 10, 11, 12, 13))
            for q in range(4):
                op("act", lambda e, q=q: e.activation(out=sg[:, q, :], in_=B[4][:, q * 128:(q + 1) * 128], func=AF.Sigmoid,
                                                      bias=cols[:, C_W0 + q:C_W0 + q + 1]), reads=bk_(4) + ["cols"], writes=zk(5))
            for q in range(4):
                op("act", lambda e, q=q: e.activation(out=a_t[:, q, :], in_=B[5][:, q * 128:(q + 1) * 128], func=AF.Sigmoid,
                                                      bias=cols[:, C_A0 + q:C_A0 + q + 1]), reads=bk_(5) + ["cols"], writes=zk(6))
            for q in range(4):
                op("dve", lambda e, q=q: e.tensor_tensor_scan(out=cs[:, q, :], data0=ones[:, 0:1].to_broadcast([128, 128]), data1=sg[:, q, :],
                                                              initial=0.0, op0=ALU.mult, op1=ALU.add), reads=["ones"] + zk(5), writes=zk(7))
            op("act", lambda e: e.activation(out=gm[:].unsqueeze(2), in_=cs[:, :, 63:64], func=AF.Exp, scale=-KAPPA), reads=zk(7), writes=["gm"])
            op("dve", lambda e: e.tensor_copy(out=c63[:].unsqueeze(2), in_=cs[:, :, 63:64]), reads=zk(7), writes=["c63"])
            op("dve", lambda e: e.tensor_tensor(out=cs, in0=cs, in1=c63[:].unsqueeze(2).to_broadcast([128, 4, 128]), op=ALU.subtract),
               reads=zk(7) + ["c63"], writes=zk(7))
            op("dve", lambda e: e.tensor_tensor(out=sg, in0=cs, in1=sg, op=ALU.subtract), reads=zk(7) + zk(5), writes=zk(5))
            op("act", lambda e: e.activation(out=E1, in_=cs, func=AF.Exp, scale=-KAPPA), reads=zk(7), writes=zk(8))
            op("act", lambda e: e.activation(out=Einv, in_=cs, func=AF.Exp, scale=KAPPA), reads=zk(7), writes=zk(9))
            E0 = sg
            op("act", lambda e: e.activation(out=E0, in_=sg, func=AF.Exp, scale=-KAPPA), reads=zk(5), writes=zk(5))
            op("dve", lambda e: e.tensor_copy(out=gC[:].unsqueeze(2), in_=E1[:, :, 127:128]), reads=zk(8), writes=["gC"])
            op("dve", lambda e: e.tensor_tensor(out=Ehat, in0=Einv, in1=gC[:].unsqueeze(2).to_broadcast([128, 4, 128]), op=ALU.mult),
               reads=["gC"] + zk(9), writes=zk(10))
            bc4 = lambda c0: cols[:, c0:c0 + 4].unsqueeze(2).to_broadcast([128, 4, 128])
            op("dve", lambda e: e.tensor_tensor(out=kk, in0=xk_, in1=bc4(C_KK), op=ALU.mult), reads=XSk + ["cols"], writes=zk(11))
            op("pool", lambda e: e.tensor_tensor(out=T1, in0=kk, in1=kk, op=ALU.mult), reads=zk(11), writes=zk(12))
            for q in range(4):
                MM(B[4][:, q * 128:(q + 1) * 128], bones[:], T1[:, q, :], True, True, reads=["bones"] + zk(12), writes=bk_(4))
            op("act", lambda e: e.activation(out=T1, in_=b4(4), func=AF.Sqrt), reads=bk_(4), writes=zk(12))
            op("dve", lambda e: e.tensor_scalar(out=T1, in0=T1, scalar1=1e-12, scalar2=None, op0=ALU.max), reads=zk(12), writes=zk(12))
            op("dve", lambda e: e.reciprocal(out=T1, in_=T1), reads=zk(12), writes=zk(12))
            op("dve", lambda e: e.tensor_tensor(out=kk, in0=kk, in1=T1, op=ALU.mult), reads=zk(11) + zk(12), writes=zk(11))
            op("dve", lambda e: e.tensor_tensor(out=bb, in0=kk, in1=a_t, op=ALU.mult), reads=zk(11) + zk(6), writes=zk(13))
            op("dve", lambda e: e.tensor_tensor(out=T1, in0=a_t, in1=bc4(C_KA), op=ALU.mult), reads=zk(6) + ["cols"], writes=zk(12))
            op("dve", lambda e: e.tensor_tensor(out=T1, in0=T1, in1=omka[:].unsqueeze(2).to_broadcast([128, 4, 128]), op=ALU.add),
               reads=zk(12) + ["omka"], writes=zk(12))
            op("dve", lambda e: e.tensor_tensor(out=T1, in0=T1, in1=xk_, op=ALU.mult), reads=zk(12) + XSk, writes=zk(12))
            kmod = T1
            PR = z2(14).rearrange("p (a b c) -> p a b c", a=4, b=2)
            BK = z2(16).rearrange("p (a b c) -> p a b c", a=4, b=2)
            op("dve", lambda e: e.scalar_tensor_tensor(out=PR[:, :, 0, :], in0=kk, scalar=-1.0, in1=E0, op0=ALU.mult, op1=ALU.mult),
               reads=zk(11) + zk(5), writes=zk(14, 16))
            op("pool", lambda e: e.tensor_tensor(out=PR[:, :, 1, :], in0=xr, in1=E1, op=ALU.mult), reads=XSk + zk(8), writes=zk(14, 16))
            op("pool", lambda e: e.tensor_tensor(out=BK[:, :, 0, :], in0=bb, in1=Einv, op=ALU.mult), reads=zk(13) + zk(9), writes=zk(16, 18))
            op("pool", lambda e: e.tensor_tensor(out=BK[:, :, 1, :], in0=kmod, in1=Einv, op=ALU.mult), reads=zk(12) + zk(9), writes=zk(16, 18))
            BhT, KhT, rk = z4(18), z4(19), z4(20)
            op("pool", lambda e: e.tensor_tensor(out=BhT, in0=bb, in1=Ehat, op=ALU.mult), reads=zk(13) + zk(10), writes=zk(18))
            op("pool", lambda e: e.tensor_tensor(out=KhT, in0=kmod, in1=Ehat, op=ALU.mult), reads=zk(12) + zk(10), writes=zk(19))
            op("dve", lambda e: e.tensor_tensor(out=rk, in0=xr, in1=kmod, op=ALU.mult), reads=XSk + zk(12), writes=zk(20))
            op("dve", lambda e: e.tensor_tensor(out=rk, in0=rk, in1=bc4(C_RK), op=ALU.mult), reads=zk(20) + ["cols"], writes=zk(20))
            for src, srck, bank, dst in [(BhT, zk(18), 6, 21), (KhT, zk(19), 7, 22), (xv, XSk, 6, 23)]:
                for q in range(4):
                    TR(B[bank][:, q * 128:(q + 1) * 128], src[:, q, :], ident[:], reads=srck + ["ident"], writes=bk_(bank))
                op("act", lambda e, bank=bank, dst=dst: e.copy(out=Z[:, dst, :], in_=B[bank][:, :]), reads=bk_(bank), writes=zk(dst))
            Bh_tm, Kh_tm, V_tm = Z[:, 21, :], Z[:, 22, :], Z[:, 23, :]
            for q in range(4):
                MM(B[7][:, 2 * q:2 * q + 2], rk[:, q, :], hsel[:], True, True, reads=zk(20) + ["hsel"], writes=bk_(7))
            op("act", lambda e: e.copy(out=sm[:, 8:16], in_=B[7][:, 0:8]), reads=bk_(7), writes=["sm8"])
            def inv_head(h):
                q, pb = h // 2, (h % 2) * 64
                k = h % 4
                bk = 4 + k
                kB = bk_(bk)
                PRh = PR[pb:pb + 64, q, :, :].rearrange("p a b -> p (a b)")
                MM(B[bk][:, 0:256], BK[pb:pb + 64, q, 0, :], PRh, True, True, reads=zk(14, 18), writes=kB)
                MM(B[bk][:, 256:512], BK[pb:pb + 64, q, 1, :], PRh, True, True, reads=zk(14, 18), writes=kB)
                op("dve", lambda e: e.tensor_tensor(out=AAk[k][1][:, 128:256], in0=B[bk][:, 0:128], in1=mSI2[:, 0:128], op=ALU.mult),
                   reads=kB + ["mSI2"], writes=["AA%d_1" % k])
                op("dve", lambda e: e.tensor_tensor(out=NA[h][:], in0=B[bk][:, 128:512], in1=mSI2[:, 128:512], op=ALU.mult),
                   reads=kB + ["mSI2"], writes=["NA%d" % h])
                yield
                MM(B[bk][:, 0:128], PR[pb:pb + 64, q, 0, :], BK[pb:pb + 64, q, 0, :], True, True, reads=zk(14, 18), writes=kB)
                op("dve", lambda e: e.tensor_tensor(out=AAk[k][1][:, 0:128], in0=B[bk][:, 0:128], in1=mSL[:], op=ALU.mult),
                   reads=kB + ["mSL"], writes=["AA%d_1" % k])
                op("pool", lambda e: e.tensor_tensor(out=TT[h][:], in0=AAk[k][1][:, 128:256], in1=ident[:], op=ALU.add),
                   reads=["AA%d_1" % k, "ident"], writes=["TT%d" % h])
                yield
                A_cur, AT_cur = AAk[k][1][:, 0:128], AAk[k][1][:, 128:256]
                kA, kAT = "AA%d_1" % k, "AA%d_1" % k
                for lvl in range(6):
                    dstA = AAk[k][lvl % 2]
                    kD = "AA%d_%d" % (k, lvl % 2)
                    MM(B[bk][:, 0:128], AT_cur, A_cur, True, True, reads=[kA, kAT], writes=kB)
                    if lvl < 5:
                        MM(B[bk][:, 128:256], A_cur, AT_cur, True, True, reads=[kA, kAT], writes=kB)
                        op("act", lambda e, dstA=dstA: e.copy(out=dstA[:], in_=B[bk][:, 0:256]), reads=kB, writes=[kD])
                    else:
                        op("act", lambda e, dstA=dstA: e.copy(out=dstA[:, 0:128], in_=B[bk][:, 0:128]), reads=kB, writes=[kD])
                    yield
                    A_cur, AT_cur = dstA[:, 0:128], dstA[:, 128:256]
                    kA = kAT = kD
                    MM(B[bk][:, 256:384], A_cur, TT[h][:], True, True, reads=[kD, "TT%d" % h], writes=kB)
                    op("dve", lambda e: e.tensor_tensor(out=TT[h][:], in0=B[bk][:, 256:384], in1=TT[h][:], op=ALU.add),
                       reads=kB + ["TT%d" % h], writes=["TT%d" % h])
                    yield

            for grp in ((0, 1, 2, 3), (4, 5, 6, 7)):
                gens = [inv_head(h) for h in grp]
                while gens:
                    for gen in list(gens):
                        try:
                            next(gen)
                        except StopIteration:
                            gens.remove(gen)
            if mid is not None:
                mid()
            op("dve", lambda e: e.tensor_tensor(out=S0m[:], in0=S[:], in1=gm[:].unsqueeze(2).to_broadcast([128, 4, 64]), op=ALU.mult),
               reads=["S", "gm"], writes=["S0m"])
            X1s, Us, Ys, Ysq, STt = Z[:, 7, :], Z[:, 9, :], Z[:, 10, :], Z[:, 11, :], Z[:, 4, :]
            P.burst_begin()
            for h in range(8):
                q, pb = h // 2, (h % 2) * 64
                hs = slice(h * 64, (h + 1) * 64)
                MM(B[4][:, hs], PR[pb:pb + 64, q, 0, :], S0m[pb:pb + 64, q, :], True, False, reads=zk(14, 16) + ["S0m"], writes=bk_(4))
                MM(B[4][:, hs], NA[h][:, 128:256], V_tm[:, hs], False, True, reads=["NA%d" % h] + zk(23), writes=bk_(4))
            P.burst_end()
            op("act", lambda e: e.copy(out=X1s, in_=B[4][:, :]), reads=bk_(4), writes=zk(7))
            P.burst_begin()
            for h in range(8):
                hs = slice(h * 64, (h + 1) * 64)
                MM(B[5][:, hs], TT[h][:], X1s[:, hs], True, True, reads=["TT%d" % h] + zk(7), writes=bk_(5))
            P.burst_end()
            op("act", lambda e: e.copy(out=Us, in_=B[5][:, :]), reads=bk_(5), writes=zk(9))
            P.burst_begin()
            for h in range(8):
                q, pb = h // 2, (h % 2) * 64
                hs = slice(h * 64, (h + 1) * 64)
                MM(B[6][:, hs], PR[pb:pb + 64, q, 1, :], S0m[pb:pb + 64, q, :], True, False, reads=zk(14, 16) + ["S0m"], writes=bk_(6))
                MM(B[6][:, hs], NA[h][:, 0:128], Us[:, hs], False, False, reads=["NA%d" % h] + zk(9), writes=bk_(6))
                MM(B[6][:, hs], NA[h][:, 256:384], V_tm[:, hs], False, True, reads=["NA%d" % h] + zk(23), writes=bk_(6))
            P.burst_end()
            op("act", lambda e: e.copy(out=Ys, in_=B[6][:, :]), reads=bk_(6), writes=zk(10))
            P.burst_begin()
            for h in range(8):
                hs = slice(h * 64, (h + 1) * 64)
                MM(B[7][0:64, hs], Bh_tm[:, hs], Us[:, hs], True, False, reads=zk(21) + zk(9), writes=bk_(7))
                MM(B[7][0:64, hs], Kh_tm[:, hs], V_tm[:, hs], False, True, reads=zk(22) + zk(23), writes=bk_(7))
            P.burst_end()
            for h in range(8):
                q, pb = h // 2, (h % 2) * 64
                hs = slice(h * 64, (h + 1) * 64)
                op("dve", lambda e, q=q, pb=pb, hs=hs: e.scalar_tensor_tensor(out=S[pb:pb + 64, q, :], in0=S0m[pb:pb + 64, q, :],
                                                                              scalar=gC[pb:pb + 64, q:q + 1], in1=B[7][0:64, hs],
                                                                              op0=ALU.mult, op1=ALU.add),
                   reads=["S0m", "gC"] + bk_(7), writes=["S"])
            dmp("Ys", Ys, zk(10), [128, 512])
            Y3 = Ys.rearrange("p (a b) -> p a b", a=8)
            op("dve", lambda e: e.tensor_reduce(out=STt[:, 0:8], in_=Y3, axis=AX.X, op=ALU.add), reads=zk(10), writes=zk(4))
            op("act", lambda e: e.activation(out=Ysq, in_=Ys, func=AF.Square), reads=zk(10), writes=zk(11))
            op("dve", lambda e: e.tensor_reduce(out=STt[:, 8:16], in_=Ysq.rearrange("p (a b) -> p a b", a=8), axis=AX.X, op=ALU.add),
               reads=zk(11), writes=zk(4))
            mean, ex2, var_, rstd_ = STt[:, 16:24], STt[:, 24:32], STt[:, 32:40], STt[:, 40:48]
            op("dve", lambda e: e.tensor_scalar(out=mean, in0=STt[:, 0:8], scalar1=1.0 / 64, scalar2=None, op0=ALU.mult), reads=zk(4), writes=zk(4))
            op("dve", lambda e: e.tensor_scalar(out=ex2, in0=STt[:, 8:16], scalar1=1.0 / 64, scalar2=None, op0=ALU.mult), reads=zk(4), writes=zk(4))
            op("dve", lambda e: e.tensor_tensor(out=var_, in0=mean, in1=mean, op=ALU.mult), reads=zk(4), writes=zk(4))
            op("dve", lambda e: e.tensor_tensor(out=var_, in0=ex2, in1=var_, op=ALU.subtract), reads=zk(4), writes=zk(4))
            op("act", lambda e: e.activation(out=rstd_, in_=var_, func=AF.Sqrt, bias=64e-5, scale=1.0), reads=zk(4), writes=zk(4))
            op("dve", lambda e: e.reciprocal(out=rstd_, in_=rstd_), reads=zk(4), writes=zk(4))
            op("dve", lambda e: e.tensor_tensor(out=Y3, in0=Y3, in1=mean.unsqueeze(2).to_broadcast([128, 8, 64]), op=ALU.subtract),
               reads=zk(10) + zk(4), writes=zk(10))
            op("dve", lambda e: e.tensor_tensor(out=Y3, in0=Y3, in1=rstd_.unsqueeze(2).to_broadcast([128, 8, 64]), op=ALU.mult),
               reads=zk(10) + zk(4), writes=zk(10))
            op("pool", lambda e: e.tensor_tensor(out=Ys, in0=Ys, in1=bcv[:, 1024:1536], op=ALU.mult), reads=zk(10) + ["bcv"], writes=zk(10))
            op("pool", lambda e: e.tensor_tensor(out=Ys, in0=Ys, in1=bcv[:, 1536:2048], op=ALU.add), reads=zk(10) + ["bcv"], writes=zk(10))
            op("dve", lambda e: e.tensor_tensor(out=Ysq.rearrange("p (a b) -> p a b", a=8), in0=V_tm.rearrange("p (a b) -> p a b", a=8),
                                                in1=sm[:, 8:16].unsqueeze(2).to_broadcast([128, 8, 64]), op=ALU.mult),
               reads=zk(23) + ["sm8"], writes=zk(11))
            op("dve", lambda e: e.tensor_tensor(out=Ys, in0=Ys, in1=Ysq, op=ALU.add), reads=zk(10) + zk(11), writes=zk(10))
            for q in range(4):
                TR(B[4][:, q * 128:(q + 1) * 128], Ys[:, q * 128:(q + 1) * 128], ident[:], reads=zk(10) + ["ident"], writes=bk_(4))
            op("dve", lambda e: e.tensor_tensor(out=ycat[:, 0:4, :], in0=b4(4), in1=Gs, op=ALU.mult), reads=bk_(4) + zk(24), writes=zk(40))
            dmp("ycat_r", ycat[:, 0:4, :], zk(40), [128, 4, 128])

        def Dsa(g, mid=None):
            c = g % NCH
            c0 = c * 128
            SC = (c + 1) * 128
            dmp = lambda name, ap, keys, shape: dump(name, ap, keys, shape, g)
            sq, kx = Z[:, 29, :], Z[:, 30, :]
            Ga = z4(37)
            op("act", lambda e: e.activation(out=Ga, in_=pT[:, 23:27, 1:129], func=AF.Silu), reads=pk(23, 27), writes=zk(37))
            op("act", lambda e: e.activation(out=sq[:, 0:128], in_=pT[:, 20, 1:129], func=AF.Square), reads=pk(20, 21), writes=zk(29))
            MM(B[0][:, 0:128], ones[:], sq[:, 0:128], True, True, reads=["ones"] + zk(29), writes=bk_(0))
            op("act", lambda e: e.activation(out=sq[:, 128:256], in_=B[0][:, 0:128], func=AF.Sqrt, bias=1e-6, scale=1.0 / 128), reads=bk_(0), writes=zk(29))
            op("dve", lambda e: e.reciprocal(out=sq[:, 128:256], in_=sq[:, 128:256]), reads=zk(29), writes=zk(29))
            op("dve", lambda e: e.scalar_tensor_tensor(out=ckvT[:, c0:c0 + 128], in0=pT[:, 20, 1:129], scalar=cols[:, C_KVG:C_KVG + 1],
                                                       in1=sq[:, 128:256], op0=ALU.mult, op1=ALU.mult),
               reads=pk(20, 21) + ["cols"] + zk(29), writes=["ckvT%d" % c])
            TR(B[1][:, 0:128], ckvT[:, c0:c0 + 128], ident[:], reads=["ckvT%d" % c, "ident"], writes=bk_(1))
            op("act", lambda e: e.copy(out=ckv_tm[:, c, :], in_=B[1][:, 0:128]), reads=bk_(1), writes=["ckvtm%d" % c])
            MM(B[0][:, 128:256], bones[:], pT[:, 21, 1:129], True, True, reads=["bones"] + pk(21, 22), writes=bk_(0))
            op("dve", lambda e: e.scalar_tensor_tensor(out=kx[:, 0:128], in0=B[0][:, 128:256], scalar=-1.0 / 64, in1=pT[:, 21, 1:129],
                                                       op0=ALU.mult, op1=ALU.add), reads=pk(21, 22) + bk_(0), writes=zk(30))
            op("act", lambda e: e.activation(out=kx[:, 128:256], in_=kx[:, 0:128], func=AF.Square), reads=zk(30), writes=zk(30))
            MM(B[0][:, 256:384], bones[:], kx[:, 128:256], True, True, reads=["bones"] + zk(30), writes=bk_(0))
            op("act", lambda e: e.activation(out=kx[:, 256:384], in_=B[0][:, 256:384], func=AF.Sqrt, bias=1e-5, scale=1.0 / 64), reads=bk_(0), writes=zk(30))
            op("dve", lambda e: e.reciprocal(out=kx[:, 256:384], in_=kx[:, 256:384]), reads=zk(30), writes=zk(30))
            op("dve", lambda e: e.tensor_tensor(out=kx[:, 0:128], in0=kx[:, 0:128], in1=kx[:, 256:384], op=ALU.mult), reads=zk(30), writes=zk(30))
            op("dve", lambda e: e.tensor_scalar(out=kidxT[:, c0:c0 + 128], in0=kx[:, 0:128], scalar1=cols[:, C_KIG:C_KIG + 1],
                                                scalar2=cols[:, C_KIB:C_KIB + 1], op0=ALU.mult, op1=ALU.add),
               reads=zk(30) + ["cols"], writes=["kidxT%d" % c])
            cq = z4(31)
            op("act", lambda e: e.activation(out=cq[:, 0:2, :], in_=pT[:, 18:20, 1:129], func=AF.Square), reads=pk(18, 20), writes=zk(31))
            for kc in range(2):
                MM(B[0][:, 384:512], ones[:], cq[:, kc, :], (kc == 0), (kc == 1), reads=["ones"] + zk(31), writes=bk_(0))
            op("act", lambda e: e.activation(out=cq[:, 0, :], in_=B[0][:, 384:512], func=AF.Sqrt, bias=1e-6, scale=1.0 / 256), reads=bk_(0), writes=zk(31))
            op("dve", lambda e: e.reciprocal(out=cq[:, 0, :], in_=cq[:, 0, :]), reads=zk(31), writes=zk(31))
            op("dve", lambda e: e.tensor_tensor(out=cq[:, 2:4, :], in0=pT[:, 18:20, 1:129], in1=cq[:, 0:1, :].to_broadcast([128, 2, 128]), op=ALU.mult),
               reads=pk(18, 20) + zk(31), writes=zk(31))
            TR(B[1][:, 256:260], pT[0:4, 22, 1:129], ident[0:4, 0:4], reads=pk(22, 23) + ["ident"], writes=bk_(1))
            op("act", lambda e: e.mul(out=wi[:], in_=B[1][:, 256:260], mul=0.5), reads=bk_(1), writes=["wi"])
            for mt in range(4):
                for kc in range(2):
                    MM(B[2][:, mt * 128:(mt + 1) * 128], wuq[:, kc, mt * 128:(mt + 1) * 128], cq[:, 2 + kc, :], (kc == 0), (kc == 1),
                       reads=["wuq"] + zk(31), writes=bk_(2))
            qT = z4(32)
            op("act", lambda e: e.copy(out=qT, in_=b4(2)), reads=bk_(2), writes=zk(32))
            for mt in range(2):
                for kc in range(2):
                    MM(B[3][:, mt * 128:(mt + 1) * 128], wqi[:, kc, mt * 128:(mt + 1) * 128], cq[:, 2 + kc, :], (kc == 0), (kc == 1),
                       reads=["wqi"] + zk(31), writes=bk_(3))
            qiT = Z[:, 35, 0:256].rearrange("p (a b) -> p a b", a=2)
            op("act", lambda e: e.copy(out=qiT, in_=B[3][:, 0:256].rearrange("p (a b) -> p a b", a=2)), reads=bk_(3), writes=zk(35))
            qa = z2(33)
            for h in range(8):
                q, pb = h // 2, (h % 2) * 64
                bk = h // 4
                MM(B[bk][:, (h % 4) * 128:(h % 4 + 1) * 128], wuk[pb:pb + 64, q, :], qT[pb:pb + 64, q, :], True, True,
                   reads=["wuk"] + zk(32), writes=bk_(bk))
            for hf in range(2):
                op("act", lambda e, hf=hf: e.mul(out=qa[:, hf * 512:(hf + 1) * 512], in_=B[hf][:, :], mul=0.125), reads=bk_(hf), writes=zk(33 + hf))
            sc = Z[:, 25:29, :].rearrange("p a b -> p (a b)")
            SCK = zk(25, 29)
            rl = Z[:, 36, :]
            kxr = ["kidxT%d" % i for i in range(c + 1)]
            for s0 in range(0, SC, 512):
                w_ = min(512, SC - s0)
                P.burst_begin()
                for hi in range(4):
                    pb = (hi % 2) * 64
                    MM(B[hi][:, 0:w_], qiT[pb:pb + 64, hi // 2, :], kidxT[pb:pb + 64, s0:s0 + w_], True, True,
                       reads=zk(35) + kxr, writes=bk_(hi))
                P.burst_end()
                for hi in range(4):
                    bk = hi
                    op("act", lambda e, bk=bk, w_=w_: e.activation(out=rl[:, 0:w_], in_=B[bk][:, 0:w_], func=AF.Relu), reads=bk_(bk), writes=zk(36))
                    if hi == 0:
                        op("dve", lambda e, s0=s0, w_=w_: e.tensor_scalar(out=sc[:, s0:s0 + w_], in0=rl[:, 0:w_], scalar1=wi[:, 0:1], scalar2=None,
                                                                          op0=ALU.mult), reads=zk(36) + ["wi"], writes=SCK)
                    else:
                        op("dve", lambda e, s0=s0, w_=w_, hi=hi: e.scalar_tensor_tensor(out=sc[:, s0:s0 + w_], in0=rl[:, 0:w_], scalar=wi[:, hi:hi + 1],
                                                                                        in1=sc[:, s0:s0 + w_], op0=ALU.mult, op1=ALU.add),
                           reads=zk(36) + ["wi"] + SCK, writes=SCK)
            op("pool", lambda e: e.affine_select(out=sc[:, c0:c0 + 128], in_=sc[:, c0:c0 + 128], pattern=[[-1, 128]], compare_op=ALU.is_ge,
                                                 fill=NEG, base=0, channel_multiplier=1), reads=SCK, writes=SCK)
            dmp("score", sc[:, 0:SC], SCK, [128, SC])
            P.mark()
            if SC > TOPK:
                for rnd in range(TOPK // 8):
                    op("dve", lambda e: e.max(out=m8[:], in_=sc[:, 0:SC]), reads=SCK, writes=["m8"], w=(TKW * (SC / 512.0) ** TKP if TKW > 0 else 1.0), dur=0.15 + SC * 0.00105)
                    op("dve", lambda e: e.match_replace(out=sc[:, 0:SC], in_to_replace=m8[:], in_values=sc[:, 0:SC], imm_value=NEG),
                       reads=SCK + ["m8"], writes=SCK, w=(TKW * (SC / 512.0) ** TKP if TKW > 0 else 1.0), dur=0.15 + SC * 0.00105)
                op("dve", lambda e: e.tensor_scalar(out=sc[:, 0:SC], in0=sc[:, 0:SC], scalar1=-1.0e29, scalar2=None, op0=ALU.is_le),
                   reads=SCK, writes=SCK)
                op("dve", lambda e: e.tensor_tensor(out=sc[:, c0:c0 + 128], in0=sc[:, c0:c0 + 128], in1=mQS[:], op=ALU.mult),
                   reads=SCK + ["mQS"], writes=SCK)
            else:
                op("dve", lambda e: e.tensor_scalar(out=sc[:, 0:SC], in0=sc[:, 0:SC], scalar1=-1.0e29, scalar2=None, op0=ALU.is_ge),
                   reads=SCK, writes=SCK)
            dmp("mask", sc[:, 0:SC], SCK, [128, SC])
            P.mark()
            for kb0 in range(0, c + 1, 4):
                nb = min(4, c + 1 - kb0)
                mbk = 2 + (kb0 // 4) % 2
                for i in range(nb):
                    kb = kb0 + i
                    TR(B[mbk][:, i * 128:(i + 1) * 128], sc[:, kb * 128:(kb + 1) * 128], ident[:], reads=SCK + ["ident"], writes=bk_(mbk))
                op("act", lambda e, kb0=kb0, nb=nb, mbk=mbk: e.copy(out=mT[:, kb0:kb0 + nb, :], in_=B[mbk][:, 0:nb * 128].rearrange("p (a b) -> p a b", a=nb)),
                   reads=bk_(mbk), writes=["mT"])
            def pv_mm(kb):
                eb = 29 + 2 * (kb % 2)
                for hf in range(2):
                    MM(B[2 + hf][:, :], ckv_tm[:, kb, :], Z[:, eb + hf, :], (kb == 0), (kb == c),
                       reads=["ckvtm%d" % kb] + zk(eb + hf), writes=bk_(2 + hf))

            def pv_acc(kb):
                eb = 29 + 2 * (kb % 2)
                for hf in range(2):
                    if kb == 0:
                        op("pool", lambda e, hf=hf, eb=eb: e.tensor_copy(out=Z[:, 38 + hf, :], in_=Z[:, eb + hf, :]), reads=zk(eb + hf), writes=zk(38 + hf))
                    else:
                        op("pool", lambda e, hf=hf, eb=eb: e.tensor_tensor(out=Z[:, 38 + hf, :], in0=Z[:, 38 + hf, :], in1=Z[:, eb + hf, :], op=ALU.add),
                           reads=zk(eb + hf) + zk(38 + hf), writes=zk(38 + hf))

            for kb in range(c + 1):
                eb = 29 + 2 * (kb % 2)
                P.burst_begin()
                for hf in range(2):
                    MM(B[hf][:, :], ckvT[:, kb * 128:(kb + 1) * 128], qa[:, hf * 512:(hf + 1) * 512], True, True,
                       reads=["ckvT%d" % kb] + zk(33, 35), writes=bk_(hf))
                if kb > 0:
                    pv_mm(kb - 1)
                P.burst_end()
                if kb > 0:
                    pv_acc(kb - 1)
                for hf in range(2):
                    op("act", lambda e, hf=hf, eb=eb: e.activation(out=Z[:, eb + hf, :], in_=B[hf][:, :], func=AF.Exp), reads=bk_(hf), writes=zk(eb + hf))
                    op(PM_ENG, lambda e, hf=hf, eb=eb, kb=kb: e.tensor_tensor(out=z4(eb + hf), in0=z4(eb + hf),
                                                                             in1=mT[:, kb:kb + 1, :].to_broadcast([128, 4, 128]), op=ALU.mult),
                       reads=zk(eb + hf) + ["mT"], writes=zk(eb + hf))
            pv_mm(c)
            pv_acc(c)
            ol = Z[:, 35:37, :]
            for hf in range(2):
                MM(B[hf][:, :], ones[:], Z[:, 38 + hf, :], True, True, reads=["ones"] + zk(38 + hf), writes=bk_(hf))
                op("dve", lambda e, hf=hf: e.reciprocal(out=Z[:, 38 + hf, :], in_=B[hf][:, :]), reads=bk_(hf), writes=zk(38 + hf))
                op("dve", lambda e, hf=hf: e.tensor_tensor(out=ol[:, hf, :], in0=B[2 + hf][:, :], in1=Z[:, 38 + hf, :], op=ALU.mult),
                   reads=bk_(2 + hf) + zk(38 + hf), writes=zk(35 + hf))
            dmp("olatT", ol, zk(35, 37), [128, 2, 512])
            for h in range(8):
                MM(B[0][:, h * 64:(h + 1) * 64], ol[:, h // 4, (h % 4) * 128:(h % 4 + 1) * 128], wuv[:, h, :], True, True,
                   reads=zk(35, 37) + ["wuv"], writes=bk_(0))
            otm = Z[:, 32, :]
            op("act", lambda e: e.copy(out=otm, in_=B[0][:, :]), reads=bk_(0), writes=zk(32))
            for q in range(4):
                TR(B[1][:, q * 128:(q + 1) * 128], otm[:, q * 128:(q + 1) * 128], ident[:], reads=zk(32) + ["ident"], writes=bk_(1))
            op("dve", lambda e: e.tensor_tensor(out=ycat[:, 4:8, :], in0=b4(1), in1=Ga, op=ALU.mult), reads=bk_(1) + zk(37), writes=zk(41))
            dmp("ycat", ycat[:], YCK, [128, 8, 128])
            op("sp", lambda e: e.dma_start(out=z2(31), in_=x_d[g * 128:(g + 1) * 128, :]), writes=zk(31, 33), dma="dma9")
            for kt in range(NWO):
                slot = (wo_ctr[0] + kt) % NWO
                op("sp", lambda e, kt=kt, slot=slot: e.dma_start(out=wosl[slot][:], in_=wout_d[:, kt, :]), writes=["wosl%d" % slot],
                   dma="dma%d" % (6 + slot))

        def O(g):
            row0 = g * 128
            xre = z2(31)
            for kt in range(8):
                slot = wo_ctr[0] % NWO
                wo_ctr[0] += 1
                if kt >= NWO:
                    op("sp", lambda e, kt=kt, slot=slot: e.dma_start(out=wosl[slot][:], in_=wout_d[:, kt, :]), writes=["wosl%d" % slot],
                       dma="dma%d" % (6 + slot))
                P.burst_begin()
                for hf in range(2):
                    MM(B[hf][:, :], ycat[:, kt, :], wosl[slot][:, hf * 512:(hf + 1) * 512], (kt == 0), (kt == 7),
                       reads=YCK + ["wosl%d" % slot], writes=bk_(hf))
                P.burst_end()
            res = Z[:, 29:31, :]
            for hf in range(2):
                op("dve", lambda e, hf=hf: e.tensor_tensor(out=res[:, hf, :], in0=B[hf][:, :], in1=xre[:, hf * 512:(hf + 1) * 512], op=ALU.add),
                   reads=bk_(hf) + zk(31, 33), writes=zk(29 + hf))
            res2 = z2(29)
            junk = z2(31)
            op("act", lambda e: e.activation(out=junk, in_=res2, func=AF.Square, accum_out=sm[:, 3:4]), reads=zk(29, 31), writes=zk(31, 33) + ["sm3"])
            op("act", lambda e: e.activation(out=sm[:, 4:5], in_=sm[:, 3:4], func=AF.Sqrt, bias=1e-6, scale=1.0 / 1024), reads=["sm3"], writes=["sm4"])
            op("dve", lambda e: e.reciprocal(out=sm[:, 5:6], in_=sm[:, 4:5]), reads=["sm4"], writes=["sm5"])
            op("dve", lambda e: e.scalar_tensor_tensor(out=junk, in0=res2, scalar=sm[:, 5:6], in1=bcv[:, 0:1024], op0=ALU.mult, op1=ALU.mult),
               reads=zk(29, 31) + ["sm5", "bcv"], writes=zk(31, 33))
            op("act", lambda e: e.dma_start(out=out_d[row0:row0 + 128, :], in_=junk), reads=zk(31, 33), dma="dma10", final=True)

        LAG = 0.0
        XP(0)
        chunks = []
        for g in range(NG):
            def streamA():
                def midA():
                    if g + 1 < NG:
                        XP(g + 1, bt=(6, 7), ba=(4, 5), zs=5)
                R(g, midA)

            def streamB():
                if g > 0:
                    O(g - 1)
                Dsa(g, None)
            chunks.append((P.record(streamA), P.record(streamB)))
        P.merge_global(chunks, LAG)
        O(NG - 1)
        P.finish()
    return nc, dbg_d


def prep_weights(inp):
    f = lambda a: np.ascontiguousarray(np.asarray(a, np.float32))
    w_in = f(inp["w_in"])[0]
    colmap = [(0, 128, False), (128, 128, False), (256, 128, False), (384, 128, False),
              (512, 128, False), (640, 128, False), (768, 128, False), (896, 128, False),
              (1024, 128, False), (1152, 128, False), (1280, 128, False), (1408, 128, False),
              (1536, 32, False), (1568, 32, False),
              (1600, 128, False), (1728, 128, False), (1856, 128, False), (1984, 128, False),
              (2112, 128, False), (2240, 128, False), (2368, 128, False),
              (2496, 64, True), (2560, 4, False),
              (2564, 128, False), (2692, 128, False), (2820, 128, False), (2948, 128, False)]
    w_in_r = np.zeros((NCT, 128, 8, 128), np.float32)
    wk = w_in.reshape(8, 128, 3076)
    for ct, (c0, n, dup) in enumerate(colmap):
        blk = wk[:, :, c0:c0 + n].transpose(1, 0, 2)
        w_in_r[ct, :, :, 0:n] = blk
        if dup:
            w_in_r[ct, :, :, 64:64 + n] = blk
    cols = np.zeros((128, NCOLS), np.float32)
    mu = f(inp["mu_shift"])[0]
    cols[:, C_MU:C_MU + 12] = mu[0:1536].reshape(12, 128).T
    cols[0:32, C_MU + 12] = mu[1536:1568]
    cols[0:32, C_MU + 13] = mu[1568:1600]
    cols[:, C_W0:C_W0 + 4] = f(inp["w0"])[0].reshape(4, 128).T
    cols[:, C_A0:C_A0 + 4] = f(inp["a0"])[0].reshape(4, 128).T
    cols[:, C_KK:C_KK + 4] = f(inp["k_k"])[0].reshape(4, 128).T
    cols[:, C_KA:C_KA + 4] = f(inp["k_a"])[0].reshape(4, 128).T
    cols[:, C_RK:C_RK + 4] = f(inp["r_k"])[0].reshape(4, 128).T
    cols[:, C_NG:C_NG + 8] = f(inp["norm_g"])[0].reshape(8, 128).T
    cols[:, C_QG:C_QG + 2] = f(inp["q_norm_g"])[0].reshape(2, 128).T
    cols[:, C_KVG] = f(inp["kv_norm_g"])[0]
    cols[:, C_KIG] = np.tile(f(inp["kidx_g"])[0], 2)
    cols[:, C_KIB] = np.tile(f(inp["kidx_b"])[0], 2)
    bcv = np.zeros((128, 2048), np.float32)
    bcv[:, 0:1024] = f(inp["final_g"])[None, :]
    bcv[:, 1024:1536] = f(inp["gn_g"])[0].reshape(1, 512)
    bcv[:, 1536:2048] = f(inp["gn_b"])[0].reshape(1, 512)
    w_uk = f(inp["w_uk"])[0]
    w_ukT_r = np.zeros((128, 4, 128), np.float32)
    for h in range(8):
        w_ukT_r[(h % 2) * 64:(h % 2) * 64 + 64, h // 2, :] = w_uk[h].T
    return {
        "w_in_r": w_in_r,
        "cols": cols,
        "bcv": bcv,
        "w_up": f(inp["w_up"])[0],
        "a_up": f(inp["a_up"])[0],
        "w_uq_r": f(f(inp["w_uq"])[0].reshape(2, 128, 512).transpose(1, 0, 2)),
        "w_qidx_r": f(f(inp["w_qidx"])[0].reshape(2, 128, 256).transpose(1, 0, 2)),
        "w_ukT_r": w_ukT_r,
        "w_uv_r": f(f(inp["w_uv"])[0].transpose(1, 0, 2)),
        "w_out_r": f(f(inp["w_out"])[0].reshape(8, 128, 1024).transpose(1, 0, 2)),
    }


def kernel(**inputs):
    x = np.asarray(inputs["x"], np.float32)
    Bsz, T, Dm = x.shape
    n = 8
    NSEQ = Bsz // n
    wts = prep_weights(inputs)
    nc, _ = build(NSEQ, T // 128)
    in_maps = []
    for i in range(n):
        m = dict(wts)
        m["x"] = np.ascontiguousarray(x[i * NSEQ:(i + 1) * NSEQ].reshape(NSEQ * T, Dm))
        in_maps.append(m)
    res = run_bass_kernel_spmd(nc, in_maps, core_ids=list(range(n)))
    out = np.concatenate([np.asarray(r["out"], np.float32).reshape(NSEQ, T, Dm) for r in res.results], axis=0)
    return out
```
